# Optimizing a Trainium2 kernel written in Bass

```python
import math
import jax, jax.numpy as jnp
from jax import lax
import numpy as np

D_MODEL = 1024
BATCH = 32
SEQ = 2048
DEPTH = 1
DEC_BATCH = 8
DEC_SEQ = 16
PAST_LEN = 2048

CHUNK = 64
QBLOCK = 128
HEAD_DIM = 64
MIX_WIDTH = D_MODEL
DIFF_WIDTH = MIX_WIDTH // 2
SB_WIDTH = MIX_WIDTH - DIFF_WIDTH
DIFF_HEADS = DIFF_WIDTH // (2 * HEAD_DIM)
DIFF_QK_DIM = 2 * HEAD_DIM
DIFF_V_DIM = 2 * HEAD_DIM
SB_HEADS = SB_WIDTH // HEAD_DIM
W_IN_COLS = 3 * DIFF_WIDTH + 3 * SB_WIDTH
D_FF = 2816
CONV_WIDTH = 3
N_BUCKETS = 32
MAX_DISTANCE = 128
EPS = 1e-6

kernel_name = "hybrid_diffattn_stickbreak_convffn_stream_step"


def rms_norm(x, w):
    xf = x.astype(jnp.float32)
    y = xf * lax.rsqrt(jnp.mean(xf * xf, axis=-1, keepdims=True) + EPS)
    return (y * w.astype(jnp.float32)).astype(x.dtype)


def rel_bucket(rel):
    half = N_BUCKETS // 2
    max_exact = half // 2
    ret = jnp.where(rel > 0, half, 0)
    n = jnp.abs(rel)
    nf = jnp.maximum(n, 1).astype(jnp.float32)
    large = max_exact + (jnp.log(nf / max_exact) / math.log(MAX_DISTANCE / max_exact)
                         * (half - max_exact)).astype(jnp.int32)
    large = jnp.minimum(large, half - 1)
    return ret + jnp.where(n < max_exact, n, large)


def _heads(t, n):
    b, s, _ = t.shape
    return t.reshape(b, s, n, -1).transpose(0, 2, 1, 3)


def _merge(t):
    b, h, s, d = t.shape
    return t.transpose(0, 2, 1, 3).reshape(b, s, h * d)


def sweep_queries(block_fn, q, qpos):
    b, h, n_q, d = q.shape
    if n_q <= QBLOCK or n_q % QBLOCK:
        return block_fn(q, qpos)
    nb = n_q // QBLOCK
    qb = q.reshape(b, h, nb, QBLOCK, d).transpose(2, 0, 1, 3, 4)
    pb = qpos.reshape(nb, QBLOCK)
    out = lax.map(lambda a: block_fn(a[0], a[1]), (qb, pb))
    return out.transpose(1, 2, 0, 3, 4).reshape(b, h, n_q, -1)


def diff_attn_block(qb, qpos, k, v, kpos, lam, rel_bias):
    scale = HEAD_DIM ** -0.5
    mask = (kpos[None, :] // CHUNK) <= (qpos[:, None] // CHUNK)
    bias = jnp.transpose(rel_bias[rel_bucket(kpos[None, :] - qpos[:, None])],
                         (2, 0, 1)).astype(jnp.float32)
    qf = qb.astype(jnp.float32) * scale
    kf = k.astype(jnp.float32)
    s1 = jnp.einsum('bhqd,bhkd->bhqk', qf[..., :HEAD_DIM], kf[..., :HEAD_DIM]) + bias
    s2 = jnp.einsum('bhqd,bhkd->bhqk', qf[..., HEAD_DIM:], kf[..., HEAD_DIM:]) + bias
    a = (jax.nn.softmax(jnp.where(mask, s1, -jnp.inf), axis=-1)
         - lam * jax.nn.softmax(jnp.where(mask, s2, -jnp.inf), axis=-1))
    return jnp.einsum('bhqk,bhkd->bhqd', a, v.astype(jnp.float32)).astype(v.dtype)


def stick_breaking_block(qb, qpos, k, v, kpos):
    scale = HEAD_DIM ** -0.5
    z = jnp.einsum('bhqd,bhkd->bhqk', qb.astype(jnp.float32) * scale, k.astype(jnp.float32))
    causal = kpos[None, :] < qpos[:, None]
    log_beta = jax.nn.log_sigmoid(z)
    log_1m = jnp.where(causal, jax.nn.log_sigmoid(-z), 0.0)
    after = lax.cumsum(log_1m, axis=3, reverse=True) - log_1m
    w = jnp.where(causal, jnp.exp(log_beta + after), 0.0)
    return jnp.einsum('bhqk,bhkd->bhqd', w, v.astype(jnp.float32)).astype(v.dtype)


def layer_forward(x, pos, past_dk, past_dv, past_sk, past_sv, past_conv,
                  attn_norm_w, w_in, lq1, lk1, lq2, lk2, subln_w, sb_norm_w, w_out,
                  ffn_norm_w, w_up, conv_w, conv_b, w_down, rel_bias, lam_init):
    b, s, _ = x.shape
    h = rms_norm(x, attn_norm_w)
    proj = jnp.einsum('bsd,de->bse', h, w_in)
    splits = np.cumsum([DIFF_WIDTH, DIFF_WIDTH, DIFF_WIDTH, SB_WIDTH, SB_WIDTH]).tolist()
    q_d, k_d, v_d, q_s, k_s, v_s = jnp.split(proj, splits, axis=-1)
    q_d, k_d, v_d = _heads(q_d, DIFF_HEADS), _heads(k_d, DIFF_HEADS), _heads(v_d, DIFF_HEADS)
    q_s, k_s, v_s = _heads(q_s, SB_HEADS), _heads(k_s, SB_HEADS), _heads(v_s, SB_HEADS)
    if past_dk is None:
        kd_all, vd_all, ks_all, vs_all, kpos = k_d, v_d, k_s, v_s, pos
        conv_prev = jnp.zeros((b, CONV_WIDTH - 1, 2 * D_FF), x.dtype)
    else:
        past_len = past_dk.shape[2]
        kd_all = jnp.concatenate([past_dk.astype(k_d.dtype), k_d], axis=2)
        vd_all = jnp.concatenate([past_dv.astype(v_d.dtype), v_d], axis=2)
        ks_all = jnp.concatenate([past_sk.astype(k_s.dtype), k_s], axis=2)
        vs_all = jnp.concatenate([past_sv.astype(v_s.dtype), v_s], axis=2)
        kpos = jnp.concatenate([jnp.arange(past_len, dtype=jnp.int32), pos])
        conv_prev = past_conv.astype(x.dtype)
    lam = (jnp.exp(jnp.sum(lq1.astype(jnp.float32) * lk1.astype(jnp.float32)))
           - jnp.exp(jnp.sum(lq2.astype(jnp.float32) * lk2.astype(jnp.float32))) + lam_init)
    o_d = sweep_queries(lambda qb, pb: diff_attn_block(qb, pb, kd_all, vd_all, kpos, lam, rel_bias),
                        q_d, pos)
    o_d = rms_norm(o_d, subln_w) * (1.0 - lam_init)
    o_s = sweep_queries(lambda qb, pb: stick_breaking_block(qb, pb, ks_all, vs_all, kpos),
                        q_s, pos)
    o_s = rms_norm(o_s, sb_norm_w)
    mix = jnp.concatenate([_merge(o_d), _merge(o_s)], axis=-1)
    x = x + jnp.einsum('bse,ed->bsd', mix, w_out)
    h2 = rms_norm(x, ffn_norm_w)
    u = jnp.einsum('bsd,df->bsf', h2, w_up)
    padded = jnp.concatenate([conv_prev, u], axis=1)
    c = conv_b
    for i in range(CONV_WIDTH):
        c = c + padded[:, i:i + s] * conv_w[i]
    gate, val = jnp.split(c, 2, axis=-1)
    x = x + jnp.einsum('bsf,fd->bsd', jax.nn.silu(gate) * val, w_down)
    new_conv = padded[:, -(CONV_WIDTH - 1):]
    return x, (k_d, v_d, k_s, v_s, new_conv)


def setup_inputs(seed: int = 0) -> dict:
    key = jax.random.key(seed)
    ks = jax.random.split(key, 24)
    f32 = jnp.float32
    nrm = lambda k, shp, sc: jax.random.normal(k, shp, f32) * sc
    return {
        "x_prompt": nrm(ks[0], (BATCH, SEQ, D_MODEL), 1.0),
        "x_sample": nrm(ks[1], (DEC_BATCH, DEC_SEQ, D_MODEL), 1.0),
        "cache_diff_k": nrm(ks[2], (DEPTH, DEC_BATCH, DIFF_HEADS, PAST_LEN, DIFF_QK_DIM), 1.0),
        "cache_diff_v": nrm(ks[3], (DEPTH, DEC_BATCH, DIFF_HEADS, PAST_LEN, DIFF_V_DIM), 1.0),
        "cache_sb_k": nrm(ks[4], (DEPTH, DEC_BATCH, SB_HEADS, PAST_LEN, HEAD_DIM), 1.0),
        "cache_sb_v": nrm(ks[5], (DEPTH, DEC_BATCH, SB_HEADS, PAST_LEN, HEAD_DIM), 1.0),
        "state_conv": nrm(ks[6], (DEPTH, DEC_BATCH, CONV_WIDTH - 1, 2 * D_FF), 1.0),
        "attn_norm_w": 1.0 + nrm(ks[7], (DEPTH, D_MODEL), 0.02),
        "w_in": nrm(ks[8], (DEPTH, D_MODEL, W_IN_COLS), D_MODEL ** -0.5),
        "lambda_q1": nrm(ks[9], (DEPTH, HEAD_DIM), 0.1),
        "lambda_k1": nrm(ks[10], (DEPTH, HEAD_DIM), 0.1),
        "lambda_q2": nrm(ks[11], (DEPTH, HEAD_DIM), 0.1),
        "lambda_k2": nrm(ks[12], (DEPTH, HEAD_DIM), 0.1),
        "diff_subln_w": 1.0 + nrm(ks[13], (DEPTH, DIFF_V_DIM), 0.02),
        "sb_norm_w": 1.0 + nrm(ks[14], (DEPTH, HEAD_DIM), 0.02),
        "w_out": nrm(ks[15], (DEPTH, MIX_WIDTH, D_MODEL), MIX_WIDTH ** -0.5),
        "ffn_norm_w": 1.0 + nrm(ks[16], (DEPTH, D_MODEL), 0.02),
        "w_up": nrm(ks[17], (DEPTH, D_MODEL, 2 * D_FF), D_MODEL ** -0.5),
        "conv_w": nrm(ks[18], (DEPTH, CONV_WIDTH, 2 * D_FF), CONV_WIDTH ** -0.5),
        "conv_b": nrm(ks[19], (DEPTH, 2 * D_FF), 0.01),
        "w_down": nrm(ks[20], (DEPTH, D_FF, D_MODEL), D_FF ** -0.5),
        "rel_bias": nrm(ks[21], (N_BUCKETS, DIFF_HEADS), 0.5),
        "final_norm_w": 1.0 + nrm(ks[22], (D_MODEL,), 0.02),
    }


def reference(x_prompt, x_sample, cache_diff_k, cache_diff_v, cache_sb_k, cache_sb_v, state_conv,
              attn_norm_w, w_in, lambda_q1, lambda_k1, lambda_q2, lambda_k2, diff_subln_w,
              sb_norm_w, w_out, ffn_norm_w, w_up, conv_w, conv_b, w_down, rel_bias, final_norm_w):
    past_len = cache_diff_k.shape[3]
    pos_p = jnp.arange(x_prompt.shape[1], dtype=jnp.int32)
    pos_s = past_len + jnp.arange(x_sample.shape[1], dtype=jnp.int32)
    xp, xs = x_prompt, x_sample
    p_states, s_states = [], []
    for l in range(DEPTH):
        lam_init = 0.8 - 0.6 * math.exp(-0.3 * l)
        w = (attn_norm_w[l], w_in[l], lambda_q1[l], lambda_k1[l], lambda_q2[l], lambda_k2[l],
             diff_subln_w[l], sb_norm_w[l], w_out[l], ffn_norm_w[l], w_up[l], conv_w[l],
             conv_b[l], w_down[l], rel_bias, lam_init)
        xp, st_p = layer_forward(xp, pos_p, None, None, None, None, None, *w)
        xs, st_s = layer_forward(xs, pos_s, cache_diff_k[l], cache_diff_v[l], cache_sb_k[l],
                                 cache_sb_v[l], state_conv[l], *w)
        p_states.append(st_p)
        s_states.append(st_s)
    y_prompt = rms_norm(xp, final_norm_w)
    y_sample = rms_norm(xs, final_norm_w)
    p_dk, p_dv, p_sk, p_sv, p_cv = [jnp.stack([st[i] for st in p_states], axis=0) for i in range(5)]
    s_dk, s_dv, s_sk, s_sv, s_cv = [jnp.stack([st[i] for st in s_states], axis=0) for i in range(5)]
    return (y_prompt, y_sample, p_dk, p_dv, p_sk, p_sv, p_cv, s_dk, s_dv, s_sk, s_sv, s_cv)
```

```python
import contextlib
import math
import numpy as np
import concourse.bass as bass
import concourse.mybir as mybir
from concourse.bass_utils import run_bass_kernel_spmd

F32 = mybir.dt.float32
BF16 = mybir.dt.bfloat16
AF = mybir.ActivationFunctionType
ALU = mybir.AluOpType

D = 1024
DFF = 2816
NJ = 22
PAST = 2048
DEC = 16
EPS = 1e-6
NEG = -30000.0
SEM_ROTATE = 30000
NWB = 3
ATTACH_WAITS = True


class Sched:
    ENGS = ("pe", "act", "dve", "pool", "sp")

    def __init__(self, nc, stack):
        self.nc = nc
        self.stack = stack
        self.eh = {"pe": nc.tensor, "act": nc.scalar, "dve": nc.vector, "pool": nc.gpsimd, "sp": nc.sync}
        self.sem = {}
        self.cnt = {}
        self.nsem = 0
        for e in self.ENGS:
            self._new_sem(e)
        self.waited = {e: {} for e in self.ENGS}
        self.lastw = {}
        self.readers = {}
        self.lanes = {}
        self.lane_of = {}

    def _new_sem(self, e):
        s = self.stack.enter_context(self.nc.semaphore("s_%s_%d" % (e, self.nsem)))
        self.nsem += 1
        self.sem[e] = s
        self.cnt[e] = 0

    def _deps(self, reads, writes):
        deps = []
        for r in reads:
            w = self.lastw.get(r)
            if w is not None:
                deps.append(w)
        for r in writes:
            w = self.lastw.get(r)
            if w is not None:
                deps.append(w)
            for x in self.readers.get(r, {}).values():
                deps.append(x)
        return deps

    def _emit_waits(self, eng, deps, skip_sem=None, attach_last=False):
        need = {}
        for (s, v) in deps:
            if skip_sem is not None and s is skip_sem:
                continue
            k = id(s)
            if k not in need or need[k][1] < v:
                need[k] = (s, v)
        wd = self.waited[eng]
        todo = []
        for k, (s, v) in need.items():
            L = self.lane_of.get(k)
            if L is not None and L[0] is s:
                v = L[1]
            if wd.get(k, 0) >= v:
                continue
            wd[k] = v
            todo.append((s, v))
        attach = None
        if attach_last and todo:
            attach = todo.pop()
        for (s, v) in todo:
            self.eh[eng].wait_ge(s, v)
        return attach

    def _mark(self, reads, writes, tok):
        for r in writes:
            self.lastw[r] = tok
            self.readers[r] = {}
        for r in reads:
            self.readers.setdefault(r, {})[id(tok[0])] = tok

    def op(self, eng, fn, reads=(), writes=()):
        deps = self._deps(reads, writes)
        skip = self.sem[eng] if eng == "pe" else None
        att = self._emit_waits(eng, deps, skip_sem=skip, attach_last=ATTACH_WAITS)
        if self.cnt[eng] >= SEM_ROTATE:
            self._new_sem(eng)
        self.cnt[eng] += 1
        tok = (self.sem[eng], self.cnt[eng])
        ins = fn(self.eh[eng])
        if att is not None:
            ins._wait_ge(att[0], att[1])
        ins.then_inc(tok[0], 1)
        self._mark(reads, writes, tok)

    def dma(self, fn, lane, reads=(), writes=(), q="sp"):
        if lane not in self.lanes:
            s = self.stack.enter_context(self.nc.semaphore("l_%s" % (lane,)))
            self.lanes[lane] = [s, 0]
            self.lane_of[id(s)] = self.lanes[lane]
        L = self.lanes[lane]
        deps = self._deps(reads, writes)
        self._emit_waits(q, deps)
        if L[1] >= SEM_ROTATE * 16:
            s = self.stack.enter_context(self.nc.semaphore("l_%s_%d" % (lane, self.nsem)))
            self.nsem += 1
            L[0] = s
            L[1] = 0
            self.lane_of[id(s)] = L
        L[1] += 16
        tok = (L[0], L[1])
        fn(self.eh[q]).then_inc(tok[0], 16)
        self._mark(reads, writes, tok)

    def finish(self, q="sp"):
        deps = [(L[0], L[1]) for L in self.lanes.values()]
        for e in self.ENGS:
            if e != q and self.cnt[e] > 0:
                deps.append((self.sem[e], self.cnt[e]))
        self._emit_waits(q, deps)


def _rel_bucket(rel):
    rel = np.asarray(rel, dtype=np.int64)
    half = 16
    max_exact = 8
    ret = np.where(rel > 0, half, 0)
    n = np.abs(rel)
    nf = np.maximum(n, 1).astype(np.float32)
    large = max_exact + (np.log(nf / np.float32(max_exact)) / np.float32(math.log(128 / max_exact))
                         * np.float32(half - max_exact)).astype(np.int32)
    large = np.minimum(large, half - 1)
    return ret + np.where(n < max_exact, n, large)


def _constants():
    oh = np.zeros((32, 3, 256), np.float32)
    i = np.arange(255)
    for t, d in enumerate((127 - i, -1 - i)):
        b = _rel_bucket(d)
        oh[b, t, i] = 1.0
        oh[15, t, i] -= 1.0
    i = np.arange(31)
    b = _rel_bucket(15 - i)
    oh[b, 2, i] = 1.0
    oh[15, 2, i] -= 1.0
    cst = np.zeros((128, 11, 128), np.float32)
    r = np.arange(128)
    cst[r, 0, r] = 1.0
    cst[r, 1, 127 - r] = 1.0
    cst[:, 2, :] = -1.0 * (r[:, None] >= r[None, :])
    cst[:, 3, :] = -1.0
    cst[:, 4, :] = 1.0
    cst[:, 5, :] = 1.0 / 128
    cst[:, 6, :] = ((r[:, None] // 64) == (r[None, :] // 64)) / 64.0
    cst[:, 7, :] = np.where((127 - r[:, None]) - r[None, :] >= 0, NEG, 0.0)
    cst[:, 8, :] = np.where((r[:, None] <= 63) & (r[None, :] < 64), NEG, 0.0)
    r16 = np.arange(16)
    cst[r16, 9, 15 - r16] = 1.0
    cst[0:16, 9, 16:32] = np.where((15 - r16[:, None]) - r16[None, :] >= 0, NEG, 0.0)
    return oh, cst


class _Stop(Exception):
    pass


def build_program(NSEQ, S, stop=None, dbg=False):
    nc = bass.Bass("TRN2", target_bir_lowering=False)

    def chk(tag):
        if stop == tag:
            raise _Stop()

    NU = S // 512
    NKB = S // 128

    def din(name, shape, dt=F32):
        return nc.dram_tensor(name, list(shape), dt, kind="ExternalInput").ap()

    def dout(name, shape):
        return nc.dram_tensor(name, list(shape), F32, kind="ExternalOutput").ap()

    xp = din("xp", [NSEQ, S, D])
    xs = din("xs", [DEC, D])
    cdk = din("cdk", [4, PAST, 128])
    cdv = din("cdv", [4, PAST, 128])
    csk = din("csk", [8, PAST, 64])
    csv = din("csv", [8, PAST, 64])
    sconv = din("sconv", [2, 2 * DFF])
    w_in = din("w_in", [D, 3072])
    w_out = din("w_out", [D, D])
    w_up = din("w_up", [D, 2 * DFF])
    w_down = din("w_down", [DFF, D])
    anw = din("anw", [D])
    fnw = din("fnw", [D])
    finw = din("finw", [D])
    sublnw = din("sublnw", [128])
    sbw = din("sbw", [64])
    lam4 = din("lam4", [4, 64])
    conv_w = din("conv_w", [3, 2 * DFF])
    conv_b = din("conv_b", [2 * DFF])
    relb = din("relb", [32, 4])
    ohc = din("ohc", [32, 768])
    cstc = din("cstc", [128, 1408])

    yp = dout("yp", [NSEQ, S, D])
    ys = dout("ys", [DEC, D])
    pdk = dout("pdk", [NSEQ, 4, S, 128])
    pdv = dout("pdv", [NSEQ, 4, S, 128])
    psk = dout("psk", [NSEQ, 8, S, 64])
    psv = dout("psv", [NSEQ, 8, S, 64])
    pcv = dout("pcv", [NSEQ, 2, 2 * DFF])
    sdk = dout("sdk", [4, DEC, 128])
    sdv = dout("sdv", [4, DEC, 128])
    ssk = dout("ssk", [8, DEC, 64])
    ssv = dout("ssv", [8, DEC, 64])
    scv = dout("scv", [2, 2 * DFF])

    if dbg:
        d_mix = nc.dram_tensor("d_mix", [128, 4096], BF16, kind="ExternalOutput").ap()
        d_x1 = nc.dram_tensor("d_x1", [512, 1024], F32, kind="ExternalOutput").ap()
        d_gT = nc.dram_tensor("d_gT", [128, NJ * 512], BF16, kind="ExternalOutput").ap()
        d_hT = nc.dram_tensor("d_hT", [128, 4096], BF16, kind="ExternalOutput").ap()
    dbg_done = [False]
    wsc = nc.dram_tensor("wsc", [25, 128, 4096], BF16, kind="Internal").ap()
    gsc = nc.dram_tensor("gsc", [4, 768], F32, kind="Internal").ap()

    def dap(t, offset, pat):
        return bass.AP(tensor=t.tensor, offset=offset, ap=[list(p) for p in pat])

    with contextlib.ExitStack() as st:
        E = st.enter_context
        SC = Sched(nc, st)
        try:

            def sb(name, shape, dt):
                return E(nc.sbuf_tensor(name, list(shape), dt))

            kT = [sb("kT%d" % i, [128, PAST + DEC], BF16) for i in range(8)]
            Vd = sb("Vd", [128, 17, 512], BF16)
            Vs = sb("Vs", [128, 17, 512], BF16)
            qz = [sb("qz%d" % i, [128, 512], BF16) for i in range(16)]
            hT = sb("hT", [128, 8, 512], BF16)
            mixT = sb("mixT", [128, 8, 512], BF16)
            gT = sb("gT", [128, NJ, 512], BF16)
            xt = [sb("xt%d" % i, [128, D], F32) for i in range(4)]
            wb = [sb("wb%d" % i, [128, 8, 512], BF16) for i in range(NWB)]
            hb = [sb("hb%d" % i, [128, D], BF16) for i in range(2)]
            stg = [sb("stg%d" % i, [128, 512], F32) for i in range(2)]
            fin_b = sb("fin_b", [128, D], F32)
            xs_stage = sb("xs_stage", [128, D], F32)
            ctmp = sb("ctmp", [128, 16], F32)
            Hb = sb("Hb", [128, 4, 2, 128], BF16)
            Hb16 = sb("Hb16", [128, 4, 16], BF16)
            cstb = sb("cstb", [128, 11, 128], BF16)
            identf = sb("identf", [128, 128], F32)
            cw = sb("cw", [128, 3, 44], F32)
            cb = sb("cb", [128, 44], F32)
            halo = sb("halo", [128, 2, 44], F32)
            anw_c = sb("anw_c", [128, 8], F32)
            fnw_c = sb("fnw_c", [128, 8], F32)
            small = sb("small", [128, 32], F32)
            lamb = sb("lamb", [128, 4, 64], F32)
            cbias = sb("cbias", [128, 4], F32)
            stat = sb("stat", [128, 16], F32)
            AR = sb("AR", [128, 21 * 256], F32)

            def ar_f32(slot, n):
                return AR[:, slot * 256: slot * 256 + n]

            def ar_bf(slot, n):
                return AR[:, slot * 256: slot * 256 + (n + 1) // 2].bitcast(BF16)

            def arr(slot, nslots):
                return ["ar%d" % (slot + i) for i in range(nslots)]

            PSA = E(nc.psum_tensor("psa", [128, 4096], F32))
            PS = [PSA[:, i * 512:(i + 1) * 512] for i in range(8)]
            PSn = ["ps%d" % i for i in range(8)]

            ident = cstb[:, 0, :]
            Jm = cstb[:, 1, :]
            negA = cstb[:, 2, :]
            negOnes = cstb[:, 3, :]
            onesb = cstb[:, 4, :]
            mean128 = cstb[:, 5, :]
            mean64 = cstb[:, 6, :]
            HsbM = cstb[:, 7, :]
            zerosb = cstb[:, 10, :]
            J16 = cstb[:, 9, 0:16]
            Hsb16 = cstb[:, 9, 16:32]

            SC.dma(lambda e: e.dma_start(out=xt[0][:, 0:1024], in_=cstc[:, 0:1024]), "xt0", writes=["xt0"])
            SC.dma(lambda e: e.dma_start(out=xt[1][:, 0:384], in_=cstc[:, 1024:1408]), "xt1", writes=["xt1"])
            SC.op("dve", lambda e: e.tensor_copy(out=cstb[:, 0:8, :], in_=xt[0][:, 0:1024].rearrange("p (a b) -> p a b", a=8)),
                  reads=["xt0"], writes=["cstb"])
            SC.op("dve", lambda e: e.tensor_copy(out=cstb[:, 8:11, :], in_=xt[1][:, 0:384].rearrange("p (a b) -> p a b", a=3)),
                  reads=["xt1"], writes=["cstb"])
            SC.op("dve", lambda e: e.tensor_copy(out=identf[:], in_=xt[0][:, 0:128]), reads=["xt0"], writes=["identf"])
            SC.dma(lambda e: e.dma_start(out=anw_c[:], in_=dap(anw, 0, [[1, 128], [128, 8]]), allow_slow_non_contiguous=True),
                   "prm", writes=["anw_c"])
            SC.dma(lambda e: e.dma_start(out=fnw_c[:], in_=dap(fnw, 0, [[1, 128], [128, 8]]), allow_slow_non_contiguous=True),
                   "prm", writes=["fnw_c"])
            for i in range(3):
                SC.dma(lambda e, i=i: e.dma_start(out=cw[:, i, :], in_=dap(conv_w, i * 2 * DFF, [[1, 128], [128, 44]]),
                                                 allow_slow_non_contiguous=True), "prm", writes=["cw"])
            SC.dma(lambda e: e.dma_start(out=cb[:], in_=dap(conv_b, 0, [[1, 128], [128, 44]]), allow_slow_non_contiguous=True),
                   "prm", writes=["cb"])
            SC.dma(lambda e: e.dma_start(out=small[:, 0:1], in_=dap(sublnw, 0, [[1, 128], [1, 1]])), "prm", writes=["small"])
            SC.dma(lambda e: e.dma_start(out=small[0:64, 1:2], in_=dap(sbw, 0, [[1, 64], [1, 1]])), "prm", writes=["small"])
            SC.dma(lambda e: e.dma_start(out=small[64:128, 1:2], in_=dap(sbw, 0, [[1, 64], [1, 1]])), "prm", writes=["small"])
            SC.dma(lambda e: e.dma_start(out=fin_b[:], in_=dap(finw, 0, [[0, 128], [1, D]])), "prm", writes=["fin_b"])
            SC.dma(lambda e: e.dma_start(out=lamb[:].rearrange("p a b -> p (a b)"), in_=dap(lam4, 0, [[0, 128], [1, 256]])),
                   "prm", writes=["lamb"])
            SC.dma(lambda e: e.dma_start(out=cbias[:], in_=dap(relb, 15 * 4, [[0, 128], [1, 4]])), "prm", writes=["cbias"])
            SC.op("pool", lambda e: e.memset(small[:, 2:3], EPS), writes=["small_eps"])
            SC.op("pool", lambda e: e.memset(halo[:], 0.0), writes=["halo"])
            for i in range(16):
                SC.op("pool", lambda e, i=i: e.memset(qz[i][:], 0.0), writes=["qz%d" % i])
            SC.op("dve", lambda e: e.tensor_tensor(out=lamb[:, 0, :], in0=lamb[:, 0, :], in1=lamb[:, 1, :], op=ALU.mult),
                  reads=["lamb"], writes=["lamb"])
            SC.op("dve", lambda e: e.tensor_tensor(out=lamb[:, 2, :], in0=lamb[:, 2, :], in1=lamb[:, 3, :], op=ALU.mult),
                  reads=["lamb"], writes=["lamb"])
            SC.op("dve", lambda e: e.tensor_scalar(out=lamb[:, 1, :], in0=lamb[:, 0, :], scalar1=1.0, scalar2=0.0, op0=ALU.mult,
                                                   op1=ALU.add, accum_out=small[:, 3:4]), reads=["lamb"], writes=["lamb", "small"])
            SC.op("dve", lambda e: e.tensor_scalar(out=lamb[:, 3, :], in0=lamb[:, 2, :], scalar1=1.0, scalar2=0.0, op0=ALU.mult,
                                                   op1=ALU.add, accum_out=small[:, 4:5]), reads=["lamb"], writes=["lamb", "small"])
            SC.op("act", lambda e: e.activation(out=small[:, 5:7], in_=small[:, 3:5], func=AF.Exp), reads=["small"], writes=["small"])
            SC.op("dve", lambda e: e.scalar_tensor_tensor(out=small[:, 7:8], in0=small[:, 6:7], scalar=-0.2, in1=small[:, 5:6],
                                                          op0=ALU.add, op1=ALU.subtract), reads=["small"], writes=["small"])
            SC.op("dve", lambda e: e.tensor_scalar_mul(out=small[:, 8:9], in0=small[:, 0:1], scalar1=0.8),
                  reads=["small"], writes=["small"])
            neglam = small[:, 7:8]
            subw8 = small[:, 8:9]
            sbwc = small[:, 1:2]
            epsc = small[:, 2:3]

            chk('consts')
            SC.dma(lambda e: e.dma_start(out=xt[2][0:32, 0:768], in_=ohc[:, :]), "xt2", writes=["xt2"])
            SC.dma(lambda e: e.dma_start(out=xt[2][0:32, 768:772], in_=relb[:, :]), "xt2", writes=["xt2"])
            SC.op("pe", lambda e: e.matmul(PS[0][0:4, 0:512], lhsT=xt[2][0:32, 768:772], rhs=xt[2][0:32, 0:512], start=True, stop=True),
                  reads=["xt2"], writes=["ps0"])
            SC.op("pe", lambda e: e.matmul(PS[1][0:4, 0:256], lhsT=xt[2][0:32, 768:772], rhs=xt[2][0:32, 512:768], start=True, stop=True),
                  reads=["xt2"], writes=["ps1"])
            SC.op("dve", lambda e: e.tensor_copy(out=xt[3][0:4, 0:512], in_=PS[0][0:4, 0:512]), reads=["ps0"], writes=["xt3"])
            SC.op("dve", lambda e: e.tensor_copy(out=xt[3][0:4, 512:768], in_=PS[1][0:4, 0:256]), reads=["ps1"], writes=["xt3"])
            SC.dma(lambda e: e.dma_start(out=gsc[:, :], in_=xt[3][0:4, 0:768]), "xt3", reads=["xt3"], writes=["gsc"])
            for h in range(4):
                for t in range(2):
                    SC.dma(lambda e, h=h, t=t: e.dma_start(out=xt[2][:, (h * 2 + t) * 128:(h * 2 + t + 1) * 128],
                                                           in_=dap(gsc, h * 768 + t * 256, [[1, 128], [1, 128]])),
                           "xt2", reads=["gsc"], writes=["xt2"])
                SC.dma(lambda e, h=h: e.dma_start(out=xt[3][0:16, 800 + h * 16: 800 + (h + 1) * 16],
                                                  in_=dap(gsc, h * 768 + 512, [[1, 16], [1, 16]])),
                       "xt3", reads=["gsc"], writes=["xt3"])
            for h in range(4):
                SC.op("dve", lambda e, h=h: e.tensor_tensor(out=Hb[:, h, 0, :], in0=xt[2][:, (h * 2) * 128:(h * 2 + 1) * 128],
                                                            in1=xt[1][:, 0:128], op=ALU.add), reads=["xt2", "xt1"], writes=["Hb"])
                SC.op("dve", lambda e, h=h: e.tensor_copy(out=Hb[:, h, 1, :], in_=xt[2][:, (h * 2 + 1) * 128:(h * 2 + 2) * 128]),
                      reads=["xt2"], writes=["Hb"])
            SC.op("pool", lambda e: e.memset(Hb16[:], 0.0), writes=["Hb16"])
            SC.op("dve", lambda e: e.tensor_copy(out=Hb16[0:16].rearrange("p a b -> p (a b)"), in_=xt[3][0:16, 800:864]),
                  reads=["xt3", "Hb16"], writes=["Hb16"])

            chk('bias')
            def piece_src(i):
                if i < 6:
                    return [(dap(w_in, i * 512, [[3072, 128], [128 * 3072, 8], [1, 512]]), None)], anw_c, 8
                if i < 8:
                    return [(dap(w_out, (i - 6) * 512, [[D, 128], [128 * D, 8], [1, 512]]), None)], None, 8
                if i < 19:
                    k = i - 8
                    return [(dap(w_up, k * 256, [[2 * DFF, 128], [128 * 2 * DFF, 8], [1, 256]]), (0, 256)),
                            (dap(w_up, DFF + k * 256, [[2 * DFF, 128], [128 * 2 * DFF, 8], [1, 256]]), (256, 512))], fnw_c, 8
                k = i - 19
                hf, jp = k // 3, k % 3
                nj = 8 if jp < 2 else 6
                return [(dap(w_down, (jp * 8 * 128) * D + hf * 512, [[D, 128], [128 * D, nj], [1, 512]]), None)], None, nj

            cast_engs = ["dve", "act"]
            ce = 0
            for i in range(25):
                srcs, scol, nj = piece_src(i)
                sslot = i % 2
                if sslot == 0:
                    views = [xt[c // 2][:, (c % 2) * 512:(c % 2) * 512 + 512] for c in range(8)]
                    alias = ["xt%d" % (c // 2) for c in range(8)]
                else:
                    gflat = gT[:].rearrange("p a b -> p (a b)")
                    views = [gflat[:, c * 1024:(c + 1) * 1024].bitcast(F32) for c in range(8)]
                    alias = ["gTa"] * 8
                for si, (src, cols) in enumerate(srcs):
                    for c in range(nj):
                        a, b_ = (0, 512) if cols is None else cols
                        wr = ["wstg%d_%d_%d" % (sslot, c, si)]
                        if i < 2:
                            wr.append(alias[c])
                        SC.dma(lambda e, src=src, c=c, a=a, b_=b_, v=views: e.dma_start(out=v[c][:, a:b_], in_=src[:, c, :]),
                               "wst%d" % sslot, writes=wr)
                wslot = i % NWB
                for c in range(nj):
                    eng = cast_engs[ce % 2]
                    ce += 1
                    rd = ["wstg%d_%d_%d" % (sslot, c, si) for si in range(len(srcs))] + [alias[c]]
                    if scol is None:
                        if eng == "act":
                            SC.op("act", lambda e, c=c, v=views, w=wslot: e.activation(out=wb[w][:, c, :], in_=v[c][:, :], func=AF.Copy),
                                  reads=rd, writes=["wbc%d_%d" % (wslot, c)])
                        else:
                            SC.op(eng, lambda e, c=c, v=views, w=wslot: e.tensor_copy(out=wb[w][:, c, :], in_=v[c][:, :]),
                                  reads=rd, writes=["wbc%d_%d" % (wslot, c)])
                    else:
                        if eng == "act":
                            SC.op("act", lambda e, c=c, v=views, w=wslot, scol=scol: e.activation(
                                out=wb[w][:, c, :], in_=v[c][:, :], func=AF.Copy, scale=scol[:, c:c + 1]),
                                reads=rd + ["anw_c", "fnw_c"], writes=["wbc%d_%d" % (wslot, c)])
                        else:
                            SC.op(eng, lambda e, c=c, v=views, w=wslot, scol=scol: e.tensor_scalar_mul(
                                out=wb[w][:, c, :], in0=v[c][:, :], scalar1=scol[:, c:c + 1]),
                                reads=rd + ["anw_c", "fnw_c"], writes=["wbc%d_%d" % (wslot, c)])
                if nj < 8:
                    SC.op("pool", lambda e, w=wslot: e.memset(wb[w][:, nj:8, :], 0.0), writes=["wbc%d_%d" % (wslot, c_) for c_ in range(nj, 8)])
                SC.dma(lambda e, i=i, w=wslot: e.dma_start(out=wsc[i, :, :], in_=wb[w][:].rearrange("p a b -> p (a b)")),
                       "wb%d" % wslot, reads=["wb%d" % wslot] + ["wbc%d_%d" % (wslot, c_) for c_ in range(8)], writes=["wsc%d" % i])

            chk('wcast')
            wstate = {"next_load": 0, "next_use": 0, "total": 0}
            seq_pieces = []

            def wload_next():
                k = wstate["next_load"]
                if k >= len(seq_pieces):
                    return
                i = seq_pieces[k]
                w = k % NWB
                SC.dma(lambda e, i=i, w=w: e.dma_start(out=wb[w][:].rearrange("p a b -> p (a b)"), in_=wsc[i, :, :]),
                       "wb%d" % w, reads=["wsc%d" % i], writes=["wb%d" % w])
                wstate["next_load"] += 1

            def wget(i):
                k = wstate["next_use"]
                assert seq_pieces[k] == i, (k, i, seq_pieces[k])
                wstate["next_use"] += 1
                return k % NWB

            def wdone():
                wload_next()

            evac_rr = [0]

            def rms_rstd(xtile, nt, xres, col, junk, jres):
                SC.op("act", lambda e: e.activation(out=junk[0:nt, 0:512], in_=xtile[0:nt, 0:512], func=AF.Square,
                                                    accum_out=stat[0:nt, col:col + 1]), reads=[xres], writes=[jres, "stat%d" % col])
                SC.op("act", lambda e: e.activation(out=junk[0:nt, 512:1024], in_=xtile[0:nt, 512:1024], func=AF.Square,
                                                    accum_out=stat[0:nt, col + 1:col + 2]), reads=[xres], writes=[jres, "stat%d" % (col + 1)])
                SC.op("dve", lambda e: e.tensor_tensor(out=stat[0:nt, col:col + 1], in0=stat[0:nt, col:col + 1],
                                                       in1=stat[0:nt, col + 1:col + 2], op=ALU.add),
                      reads=["stat%d" % col, "stat%d" % (col + 1)], writes=["stat%d" % col])
                SC.op("act", lambda e: e.activation(out=stat[0:nt, col:col + 1], in_=stat[0:nt, col:col + 1], func=AF.Ln,
                                                    scale=1.0 / D, bias=epsc[0:nt, :]), reads=["stat%d" % col, "small_eps"], writes=["stat%d" % col])
                SC.op("act", lambda e: e.activation(out=stat[0:nt, col:col + 1], in_=stat[0:nt, col:col + 1], func=AF.Exp, scale=-0.5),
                      reads=["stat%d" % col], writes=["stat%d" % col])

            def norm_A(xtile, xres, nt, hbi, col):
                hbt = hb[hbi]
                hres = "hb%d" % hbi
                rms_rstd(xtile, nt, xres, col, hbt, hres)
                SC.op("dve", lambda e: e.tensor_scalar_mul(out=hbt[0:nt, :], in0=xtile[0:nt, :], scalar1=stat[0:nt, col:col + 1]), reads=[xres, "stat%d" % col], writes=[hres])

            def norm_B(tok0, nt, hbi, dstT, dres):
                hbt = hb[hbi]
                hres = "hb%d" % hbi
                pT = PS[7][:].bitcast(BF16).rearrange("p (a b) -> p a b", a=8)
                for c in range(8):
                    SC.op("pe", lambda e, c=c: e.transpose(out=pT[:, c, 0:nt], in_=hbt[0:nt, c * 128:(c + 1) * 128],
                                                           identity=cstb[0:nt, 0, 0:nt]),
                          reads=[hres, "cstb"], writes=["ps7"])
                SC.op("dve", lambda e: e.tensor_copy(out=dstT[:, :, tok0:tok0 + nt], in_=pT[:, :, 0:nt]), reads=["ps7"], writes=[dres])

            def norm_transpose(xtile, xres, tok0, nt, hbi, col, dstT, dres):
                norm_A(xtile, xres, nt, hbi, col)
                norm_B(tok0, nt, hbi, dstT, dres)

            def phase1_A(U, ti):
                tok0, nt = U.tiles[ti]
                SC.dma(lambda e: e.dma_start(out=xs_stage[0:nt, :], in_=U.x_src(tok0, nt)), "xs_stage", writes=["xs_stage"])
                norm_A(xs_stage, "xs_stage", nt, ti % 2, 2 * ti)

            def phase1_B(U, ti):
                tok0, nt = U.tiles[ti]
                norm_B(tok0, nt, ti % 2, mixT, "mixT")

            def phase1_tile(U, ti):
                phase1_A(U, ti)
                phase1_B(U, ti)

            class Unit:
                pass

            def run_unit(U, nextU):
                T = U.T
                tiles = U.tiles
                for ti, (tok0, nt) in enumerate(tiles):
                    SC.dma(lambda e, ti=ti, tok0=tok0, nt=nt: e.dma_start(out=xt[ti][0:nt, :], in_=U.x_src(tok0, nt)),
                           "xt%d" % ti, writes=["xt%d" % ti])
                hsrc = mixT

                chk('p1')
                def feat_major(w, dst_list, kcol, scale):
                    for gi in range(4):
                        bank = evac_rr[0] % 2
                        evac_rr[0] += 1
                        for c in range(8):
                            SC.op("pe", lambda e, c=c, gi=gi, bank=bank: e.matmul(
                                PS[bank][:, 0:T], lhsT=wb[w][:, c, gi * 128:(gi + 1) * 128], rhs=hsrc[:, c, 0:T],
                                start=(c == 0), stop=(c == 7)), reads=["wb%d" % w, "mixT"], writes=[PSn[bank]])
                        for di, (dst, dres, p0, p1) in enumerate(dst_list[gi]):
                            if (bank + di) % 2 == 0:
                                SC.op("act", lambda e, dst=dst, bank=bank, p0=p0, p1=p1: e.activation(
                                    out=dst[p0:p1, kcol:kcol + T], in_=PS[bank][p0:p1, 0:T], func=AF.Copy, scale=scale),
                                    reads=[PSn[bank]], writes=[dres])
                            else:
                                SC.op("dve", lambda e, dst=dst, bank=bank, p0=p0, p1=p1: e.tensor_scalar_mul(
                                    out=dst[p0:p1, kcol:kcol + T], in0=PS[bank][p0:p1, 0:T], scalar1=scale),
                                    reads=[PSn[bank]], writes=[dres])

                def tok_major(w, out_fn, vdst):
                    for ti, (tok0, nt) in enumerate(tiles):
                        bank = evac_rr[0] % 2
                        evac_rr[0] += 1
                        for c in range(8):
                            SC.op("pe", lambda e, c=c, bank=bank, tok0=tok0, nt=nt: e.matmul(
                                PS[bank][0:nt, 0:512], lhsT=hsrc[:, c, tok0:tok0 + nt], rhs=wb[w][:, c, :],
                                start=(c == 0), stop=(c == 7)), reads=["wb%d" % w, "mixT"], writes=[PSn[bank]])
                        sg = stg[bank]
                        SC.op("act", lambda e, bank=bank, nt=nt, sg=sg: e.activation(out=sg[0:nt, :], in_=PS[bank][0:nt, 0:512], func=AF.Copy),
                              reads=[PSn[bank]], writes=["stg%d" % bank])
                        if vdst is not None:
                            vt, vres = vdst
                            kb = U.kb0 + ti
                            SC.op("pool", lambda e, nt=nt, sg=sg, kb=kb, vt=vt: e.tensor_copy(out=vt[0:nt, kb, :], in_=sg[0:nt, :]),
                                  reads=["stg%d" % bank], writes=[vres])
                        for (dst, src) in out_fn(tok0, nt, sg):
                            SC.dma(lambda e, dst=dst, src=src: e.dma_start(out=dst, in_=src), "o_stg%d" % bank, reads=["stg%d" % bank], q="pool")

                w = wget(0)
                feat_major(w, [[(qz[2 * h], "qz%d" % (2 * h), 0, 64), (qz[2 * h + 1], "qz%d" % (2 * h + 1), 64, 128)] for h in range(4)], 0, 0.125)
                wdone()
                w = wget(1)
                feat_major(w, [[(kT[h], "kT%d" % h, 0, 128)] for h in range(4)], U.kpos0, 1.0)
                tok_major(w, U.out_dk, None)
                wdone()
                w = wget(2)
                tok_major(w, U.out_dv, (Vd, "Vd"))
                wdone()
                w = wget(3)
                feat_major(w, [[(qz[8 + 2 * p], "qz%d" % (8 + 2 * p), 0, 64), (qz[9 + 2 * p], "qz%d" % (9 + 2 * p), 64, 128)] for p in range(4)], 0, 0.125)
                wdone()
                w = wget(4)
                feat_major(w, [[(kT[4 + p], "kT%d" % (4 + p), 0, 128)] for p in range(4)], U.kpos0, 1.0)
                tok_major(w, U.out_sk, None)
                wdone()
                w = wget(5)
                tok_major(w, U.out_sv, (Vs, "Vs"))
                wdone()

                chk('p2')
                W = min(256, T)
                nsub = T // W
                Sbk = [0, 1, 4, 5]
                Sv = [PS[b_][:, 0:2 * W].rearrange("p (a b) -> p a b", a=2) for b_ in Sbk]
                Sbanks = [[PSn[b_]] for b_ in Sbk]
                NSB = 3 if T >= 256 else 2
                ODbanks = [(2, 3), (6, 7)]
                pend = [None]

                def diff_job(ji, h, u):
                    q0 = u * W
                    blocks = U.diff_blocks(q0, W, h)
                    ob, db = ODbanks[ji % 2]
                    Ov = PS[ob][:, 0:2 * W].rearrange("p (a b) -> p a b", a=2)
                    Dv = PS[db][:, 0:2 * W].rearrange("p (a b) -> p a b", a=2)
                    nb = len(blocks)

                    def stage1(bi):
                        kb, ksz, kcol, c0, near = blocks[bi]
                        sbk = bi % NSB
                        pbk = [0, 2, 12][bi % NSB]
                        Sb = Sv[sbk]
                        SC.op("pe", lambda e: e.matmul(
                            Sb[0:ksz, 0, c0:W], lhsT=kT[h][:, kcol:kcol + ksz], rhs=qz[2 * h][:, q0 + c0:q0 + W],
                            start=True, stop=False, skip_group_check=True),
                            reads=["kT%d" % h, "qz%d" % (2 * h)], writes=Sbanks[sbk])
                        SC.op("pe", lambda e: e.matmul(
                            Sb[0:ksz, 1, c0:W], lhsT=kT[h][:, kcol:kcol + ksz], rhs=qz[2 * h + 1][:, q0 + c0:q0 + W],
                            start=False, stop=False, skip_group_check=True),
                            reads=["kT%d" % h, "qz%d" % (2 * h + 1)], writes=Sbanks[sbk])
                        for (coff, wd, Ht, Jt) in near:
                            for half in range(2):
                                SC.op("pe", lambda e, coff=coff, wd=wd, Ht=Ht, Jt=Jt, half=half: e.matmul(
                                    Sb[0:ksz, half, coff:coff + wd], lhsT=Jt, rhs=Ht, start=False, stop=False, skip_group_check=True),
                                    reads=["Hb", "Hb16", "cstb"], writes=Sbanks[sbk])
                        Pt = ar_bf(pbk, 2 * W).rearrange("p (a b) -> p a b", a=2)
                        SC.op("act", lambda e: e.activation(
                            out=Pt[0:ksz, :, c0:W], in_=Sb[0:ksz, :, c0:W], func=AF.Exp, bias=cbias[0:ksz, h:h + 1]),
                            reads=Sbanks[sbk] + ["cbias"], writes=arr(pbk, 2))

                    def stage2(bi):
                        kb, ksz, kcol, c0, near = blocks[bi]
                        pbk = [0, 2, 12][bi % NSB]
                        Pt = ar_bf(pbk, 2 * W).rearrange("p (a b) -> p a b", a=2)
                        pres = arr(pbk, 2)
                        first = (bi == 0)
                        for half in range(2):
                            SC.op("pe", lambda e, half=half: e.matmul(
                                Ov[:, half, c0:W], lhsT=Vd[0:ksz, kb, h * 128:(h + 1) * 128], rhs=Pt[0:ksz, half, c0:W],
                                start=(first and half == 0), stop=False, skip_group_check=True),
                                reads=pres + ["Vd"], writes=[PSn[ob]])
                        for half in range(2):
                            SC.op("pe", lambda e, half=half: e.matmul(
                                Dv[:, half, c0:W], lhsT=onesb[0:ksz, :], rhs=Pt[0:ksz, half, c0:W],
                                start=(first and half == 0), stop=False, skip_group_check=True),
                                reads=pres + ["cstb"], writes=[PSn[db]])

                    stage1(0)
                    if nb > 1 and NSB == 3:
                        stage1(1)
                    for bi in range(nb):
                        nxt = bi + 2 if NSB == 3 else bi + 1
                        if nxt < nb:
                            stage1(nxt)
                        stage2(bi)
                        if pend[0] is not None and (bi == 1 or bi == nb - 1):
                            pend[0]()
                            pend[0] = None
                    rD = ar_f32(4, 2 * W).rearrange("p (a b) -> p a b", a=2)
                    on = ar_f32(6, 2 * W).rearrange("p (a b) -> p a b", a=2)
                    od = ar_f32(8, W)
                    sq = ar_bf(9, W)
                    rr = ar_f32(10, W)
                    SC.op("dve", lambda e: e.reciprocal(out=rD[:, :, :], in_=Dv[:, :, :]), reads=[PSn[db]], writes=arr(4, 2))
                    SC.op("dve", lambda e: e.tensor_tensor(out=on[:, :, :], in0=Ov[:, :, :], in1=rD[:, :, :], op=ALU.mult),
                          reads=[PSn[ob]] + arr(4, 2), writes=arr(6, 2))
                    SC.op("dve", lambda e: e.scalar_tensor_tensor(out=od[:, :], in0=on[:, 1, :], scalar=neglam,
                                                                  in1=on[:, 0, :], op0=ALU.mult, op1=ALU.add),
                          reads=arr(6, 2) + ["small"], writes=arr(8, 1))
                    SC.op("pool", lambda e: e.tensor_tensor(out=sq[:, :], in0=od[:, :], in1=od[:, :], op=ALU.mult),
                          reads=arr(8, 1), writes=arr(9, 1))

                    def epiB():
                        SC.op("pe", lambda e: e.matmul(PS[db][:, 0:W], lhsT=mean128, rhs=sq[:, :], start=True, stop=True),
                              reads=arr(9, 1) + ["cstb"], writes=[PSn[db]])
                        SC.op("act", lambda e: e.activation(out=rr[:, :], in_=PS[db][:, 0:W], func=AF.Ln, bias=epsc),
                              reads=[PSn[db], "small_eps"], writes=arr(10, 1))
                        SC.op("act", lambda e: e.activation(out=rr[:, :], in_=rr[:, :], func=AF.Exp, scale=-0.5),
                              reads=arr(10, 1), writes=arr(10, 1))
                        SC.op("dve", lambda e: e.scalar_tensor_tensor(
                            out=mixT[:, h, q0:q0 + W], in0=od[:, :], scalar=subw8, in1=rr[:, :], op0=ALU.mult, op1=ALU.mult),
                            reads=arr(8, 1) + arr(10, 1) + ["small"], writes=["mixT"])
                    pend[0] = epiB

                ji = 0
                for h in range(4):
                    for u in range(nsub):
                        diff_job(ji, h, u)
                        ji += 1
                if pend[0] is not None:
                    pend[0]()
                    pend[0] = None

                chk('p3a')
                Lsum = ar_bf(20, 512)
                for p in range(4):
                    for ehd in range(2):
                        hh = 2 * p + ehd
                        pr0 = ehd * 64
                        blocks = U.sb_blocks()
                        SC.op("pool", lambda e: e.memset(Lsum[:, :], 0.0), writes=arr(20, 1))
                        nb = len(blocks)

                        def s1(bi):
                            kb, ksz, kcol, c0, diag = blocks[bi]
                            zb = bi % 2
                            eb = [0, 2, 18][bi % 3]
                            SC.op("pe", lambda e: e.matmul(PS[zb][0:ksz, c0:T], lhsT=kT[4 + p][:, kcol:kcol + ksz],
                                                           rhs=qz[8 + hh][:, c0:T], start=True, stop=False, skip_group_check=True),
                                  reads=["kT%d" % (4 + p), "qz%d" % (8 + hh)], writes=[PSn[zb]])
                            if diag is not None:
                                dw, Ht, Jt = diag
                                SC.op("pe", lambda e: e.matmul(PS[zb][0:ksz, c0:c0 + dw], lhsT=Jt, rhs=Ht, start=False, stop=False,
                                                               skip_group_check=True), reads=["cstb"], writes=[PSn[zb]])
                            Et = ar_f32(eb, 512)
                            Lp = ar_bf(4 + zb, 512)
                            SC.op("act", lambda e: e.activation(out=Et[0:ksz, c0:T], in_=PS[zb][0:ksz, c0:T], func=AF.Exp),
                                  reads=[PSn[zb]], writes=arr(eb, 2))
                            SC.op("act", lambda e: e.activation(out=Lp[0:ksz, c0:T], in_=Et[0:ksz, c0:T], func=AF.Ln, bias=1.0),
                                  reads=arr(eb, 2), writes=arr(4 + zb, 1))

                        def s2a(bi):
                            kb, ksz, kcol, c0, diag = blocks[bi]
                            zb = bi % 2
                            eb = [0, 2, 18][bi % 3]
                            cbk = 2 + zb
                            Et = ar_f32(eb, 512)
                            Lp = ar_bf(4 + zb, 512)
                            Wt = ar_bf(6 + zb, 512)
                            Xt = ar_f32(14 + 2 * zb, 512)
                            SC.op("pe", lambda e: e.matmul(PS[cbk][0:ksz, c0:T], lhsT=negA[0:ksz, 0:ksz], rhs=Lp[0:ksz, c0:T],
                                                           start=True, stop=False, skip_group_check=True),
                                  reads=arr(4 + zb, 1) + ["cstb"], writes=[PSn[cbk]])
                            if diag is not None:
                                dw, Ht, Jt = diag
                                SC.op("pe", lambda e: e.matmul(PS[cbk][0:ksz, c0:c0 + dw], lhsT=Jt, rhs=Ht, start=False, stop=False,
                                                               skip_group_check=True), reads=["cstb"], writes=[PSn[cbk]])
                            if bi > 0:
                                SC.op("pe", lambda e: e.matmul(PS[cbk][0:ksz, c0:T], lhsT=negOnes[:, 0:ksz], rhs=Lsum[:, c0:T],
                                                               start=False, stop=False, skip_group_check=True),
                                      reads=arr(20, 1) + ["cstb"], writes=[PSn[cbk]])
                            if bi + 1 < nb:
                                SC.op("pool", lambda e: e.tensor_tensor(out=Lsum[0:ksz, c0:T], in0=Lsum[0:ksz, c0:T], in1=Lp[0:ksz, c0:T],
                                                                        op=ALU.add), reads=arr(20, 1) + arr(4 + zb, 1), writes=arr(20, 1))
                            SC.op("act", lambda e: e.activation(out=Xt[0:ksz, c0:T], in_=PS[cbk][0:ksz, c0:T], func=AF.Exp),
                                  reads=[PSn[cbk]], writes=arr(14 + 2 * zb, 2))
                            SC.op("dve", lambda e: e.tensor_tensor(out=Wt[0:ksz, c0:T], in0=Et[0:ksz, c0:T], in1=Xt[0:ksz, c0:T], op=ALU.mult),
                                  reads=arr(eb, 2) + arr(14 + 2 * zb, 2), writes=arr(6 + zb, 1))

                        def s2b(bi):
                            kb, ksz, kcol, c0, diag = blocks[bi]
                            zb = bi % 2
                            Wt = ar_bf(6 + zb, 512)
                            SC.op("pe", lambda e: e.matmul(PS[4][pr0:pr0 + 64, c0:T], lhsT=Vs[0:ksz, kb, hh * 64:(hh + 1) * 64],
                                                           rhs=Wt[0:ksz, c0:T], start=(bi == 0), stop=False, skip_group_check=True),
                                  reads=arr(6 + zb, 1) + ["Vs"], writes=["ps4"])

                        s1(0)
                        if nb > 1:
                            s1(1)
                        s2a(0)
                        for bi in range(nb):
                            if bi + 2 < nb:
                                s1(bi + 2)
                            if bi + 1 < nb:
                                s2a(bi + 1)
                            s2b(bi)
                    osb = ar_f32(8, 512)
                    sq = ar_bf(10, 512)
                    rr = ar_f32(11, 512)
                    SC.op("dve", lambda e: e.tensor_copy(out=osb[:, 0:T], in_=PS[4][:, 0:T]), reads=["ps4"], writes=arr(8, 2))
                    SC.op("pool", lambda e: e.tensor_tensor(out=sq[:, 0:T], in0=osb[:, 0:T], in1=osb[:, 0:T], op=ALU.mult),
                          reads=arr(8, 2), writes=arr(10, 1))
                    SC.op("pe", lambda e: e.matmul(PS[6][:, 0:T], lhsT=mean64, rhs=sq[:, 0:T], start=True, stop=True),
                          reads=arr(10, 1) + ["cstb"], writes=["ps6"])
                    SC.op("act", lambda e: e.activation(out=rr[:, 0:T], in_=PS[6][:, 0:T], func=AF.Ln, bias=epsc),
                          reads=["ps6", "small_eps"], writes=arr(11, 2))
                    SC.op("act", lambda e: e.activation(out=rr[:, 0:T], in_=rr[:, 0:T], func=AF.Exp, scale=-0.5),
                          reads=arr(11, 2), writes=arr(11, 2))
                    SC.op("dve", lambda e, p=p: e.scalar_tensor_tensor(out=mixT[:, 4 + p, 0:T], in0=osb[:, 0:T], scalar=sbwc, in1=rr[:, 0:T],
                                                                      op0=ALU.mult, op1=ALU.mult),
                          reads=arr(8, 2) + arr(11, 2) + ["small"], writes=["mixT"])

                chk('p3b')
                if dbg and not dbg_done[0]:
                    SC.dma(lambda e: e.dma_start(out=d_mix[:, :], in_=mixT[:].rearrange("p a b -> p (a b)")), "dbg", reads=["mixT"])
                wA = wget(6)
                wB = wget(7)
                prev = None
                for ti, (tok0, nt) in enumerate(tiles):
                    for half in range(2):
                        w = wA if half == 0 else wB
                        bank = 2 + half
                        for c in range(8):
                            SC.op("pe", lambda e, c=c, w=w, bank=bank, tok0=tok0, nt=nt: e.matmul(
                                PS[bank][0:nt, 0:512], lhsT=mixT[:, c, tok0:tok0 + nt], rhs=wb[w][:, c, :],
                                start=(c == 0), stop=(c == 7)), reads=["wb%d" % w, "mixT"], writes=[PSn[bank]])
                        SC.op("dve", lambda e, ti=ti, nt=nt, bank=bank, half=half: e.tensor_tensor(
                            out=xt[ti][0:nt, half * 512:(half + 1) * 512], in0=PS[bank][0:nt, 0:512],
                            in1=xt[ti][0:nt, half * 512:(half + 1) * 512], op=ALU.add),
                            reads=[PSn[bank], "xt%d" % ti], writes=["xt%d" % ti])
                    if prev is not None:
                        norm_transpose(xt[prev[0]], "xt%d" % prev[0], prev[1], prev[2], prev[0] % 2, 2 * prev[0], hT, "hT")
                    prev = (ti, tok0, nt)
                norm_transpose(xt[prev[0]], "xt%d" % prev[0], prev[1], prev[2], prev[0] % 2, 2 * prev[0], hT, "hT")
                wdone()
                wdone()

                chk('p4')
                if dbg and not dbg_done[0]:
                    for ti in range(4):
                        SC.dma(lambda e, ti=ti: e.dma_start(out=d_x1[ti * 128:(ti + 1) * 128, :], in_=xt[ti][:, :]), "dbg", reads=["xt%d" % ti])
                    SC.dma(lambda e: e.dma_start(out=d_hT[:, :], in_=hT[:].rearrange("p a b -> p (a b)")), "dbg", reads=["hT"])
                for k in range(11):
                    w = wget(8 + k)
                    for s_ in range(2):
                        j = 2 * k + s_
                        par = j % 2
                        bufs = []
                        for gv in range(2):
                            ch = j + gv * NJ
                            bank = gv + 4 * par
                            for c in range(8):
                                SC.op("pe", lambda e, c=c, bank=bank, gv=gv: e.matmul(
                                    PS[bank][:, 0:T], lhsT=wb[w][:, c, gv * 256 + s_ * 128: gv * 256 + (s_ + 1) * 128], rhs=hT[:, c, 0:T],
                                    start=(c == 0), stop=(c == 7)), reads=["wb%d" % w, "hT"], writes=[PSn[bank]])
                            s0 = par * 10 + gv * 3
                            us = AR[:, s0 * 256:s0 * 256 + 2 + T]
                            ures = arr(s0, 3)
                            c0_ = par * 10 + 6 + gv * 2
                            cc = ar_f32(c0_, 512)
                            cres = arr(c0_, 2)
                            SC.op("pool", lambda e, us=us, ch=ch: e.tensor_copy(out=us[:, 0:2], in_=halo[:, :, ch]),
                                  reads=["halo"], writes=ures)
                            SC.op("act", lambda e, us=us, bank=bank: e.activation(out=us[:, 2:2 + T], in_=PS[bank][:, 0:T], func=AF.Copy),
                                  reads=[PSn[bank]], writes=ures)
                            SC.op("act", lambda e, cc=cc, bank=bank, ch=ch: e.activation(
                                out=cc[:, 0:T], in_=PS[bank][:, 0:T], func=AF.Identity, scale=cw[:, 2, ch:ch + 1], bias=cb[:, ch:ch + 1]),
                                reads=[PSn[bank], "cw", "cb"], writes=cres)
                            SC.op("pool", lambda e, us=us, ch=ch: e.tensor_copy(out=halo[:, :, ch], in_=us[:, T:T + 2]),
                                  reads=ures, writes=["halo"])
                            SC.op("dve", lambda e, us=us, cc=cc, ch=ch: e.scalar_tensor_tensor(
                                out=cc[:, 0:T], in0=us[:, 1:1 + T], scalar=cw[:, 1, ch:ch + 1], in1=cc[:, 0:T],
                                op0=ALU.mult, op1=ALU.add), reads=ures + cres + ["cw"], writes=cres)
                            SC.op("dve", lambda e, us=us, cc=cc, ch=ch: e.scalar_tensor_tensor(
                                out=cc[:, 0:T], in0=us[:, 0:T], scalar=cw[:, 0, ch:ch + 1], in1=cc[:, 0:T],
                                op0=ALU.mult, op1=ALU.add), reads=ures + cres + ["cw"], writes=cres)
                            bufs.append((cc, cres))
                        (cg, gres), (cv, vres) = bufs
                        SC.op("act", lambda e, cg=cg: e.activation(out=cg[:, 0:T], in_=cg[:, 0:T], func=AF.Silu), reads=gres, writes=gres)
                        SC.op("pool", lambda e, cg=cg, cv=cv, j=j: e.tensor_tensor(out=gT[:, j, 0:T], in0=cg[:, 0:T], in1=cv[:, 0:T], op=ALU.mult),
                              reads=gres + vres, writes=["gTa"])
                    wdone()

                chk('p5')
                if dbg and not dbg_done[0]:
                    SC.dma(lambda e: e.dma_start(out=d_gT[:, :], in_=gT[:].rearrange("p a b -> p (a b)")), "dbg", reads=["gTa"])
                    dbg_done[0] = True
                for hf in range(2):
                    for jp in range(3):
                        w = wget(19 + hf * 3 + jp)
                        nj = 8 if jp < 2 else 6
                        pi_ = hf * 3 + jp
                        if nextU is not None and pi_ < len(nextU.tiles):
                            phase1_A(nextU, pi_)
                        for ti, (tok0, nt) in enumerate(tiles):
                            bank = 2 + ti
                            for jl in range(nj):
                                j = jp * 8 + jl
                                SC.op("pe", lambda e, w=w, jl=jl, j=j, bank=bank, tok0=tok0, nt=nt, jp=jp, nj=nj: e.matmul(
                                    PS[bank][0:nt, 0:512], lhsT=gT[:, j, tok0:tok0 + nt], rhs=wb[w][:, jl, :],
                                    start=(jp == 0 and jl == 0), stop=(jp == 2 and jl == nj - 1)),
                                    reads=["wb%d" % w, "gTa"], writes=[PSn[bank]])
                        wdone()
                        if nextU is not None and pi_ < len(nextU.tiles):
                            phase1_B(nextU, pi_)
                    for ti, (tok0, nt) in enumerate(tiles):
                        bank = 2 + ti
                        SC.op("dve", lambda e, ti=ti, nt=nt, bank=bank, hf=hf: e.tensor_tensor(
                            out=xt[ti][0:nt, hf * 512:(hf + 1) * 512], in0=PS[bank][0:nt, 0:512],
                            in1=xt[ti][0:nt, hf * 512:(hf + 1) * 512], op=ALU.add),
                            reads=[PSn[bank], "xt%d" % ti], writes=["xt%d" % ti])
                for ti, (tok0, nt) in enumerate(tiles):
                    col = 8 + 2 * ti
                    rms_rstd(xt[ti], nt, "xt%d" % ti, col, hb[ti % 2], "hb%d" % (ti % 2))
                    SC.op("dve", lambda e, ti=ti, nt=nt, col=col: e.scalar_tensor_tensor(
                        out=xt[ti][0:nt, :], in0=xt[ti][0:nt, :], scalar=stat[0:nt, col:col + 1], in1=fin_b[0:nt, :],
                        op0=ALU.mult, op1=ALU.mult), reads=["xt%d" % ti, "stat%d" % col, "fin_b"], writes=["xt%d" % ti])
                    SC.dma(lambda e, ti=ti, tok0=tok0, nt=nt: e.dma_start(out=U.y_dst(tok0, nt), in_=xt[ti][0:nt, :]),
                           "o_xt%d" % ti, reads=["xt%d" % ti], q="pool")

            def conv_state_out(dst):
                hv = halo[:].rearrange("p a b -> p (a b)")
                SC.op("pe", lambda e: e.transpose(out=PS[5][0:88, 0:128], in_=hv[:, 0:88], identity=identf[:, :]),
                      reads=["halo", "identf"], writes=["ps5"])
                cs = ar_f32(12, 128)
                SC.op("dve", lambda e: e.tensor_copy(out=cs[0:88, :], in_=PS[5][0:88, 0:128]), reads=["ps5"], writes=arr(12, 1))
                for i in range(2):
                    SC.dma(lambda e, i=i: e.dma_start(out=dst[i, :].rearrange("(a b) -> a b", b=128), in_=cs[i * 44:(i + 1) * 44, :]),
                           "cso", reads=arr(12, 1))

            n_units = NSEQ * NU + 1
            for _ in range(n_units):
                seq_pieces.extend(range(25))
            for _ in range(NWB):
                wload_next()

            def prompt_near(h, kb, qb0, nqb):
                near = []
                for ty in range(2):
                    qb = kb + ty
                    if qb0 <= qb < qb0 + nqb:
                        near.append(((qb - qb0) * 128, 128, Hb[:, h, ty, :], Jm))
                return near

            units = []
            for b in range(NSEQ):
                for G in range(NU):
                    U = Unit()
                    U.T = 512
                    U.tiles = [(i * 128, 128) for i in range(4)]
                    U.kpos0 = G * 512
                    U.kb0 = G * 4
                    U.x_src = lambda tok0, nt, b=b, G=G: xp[b, G * 512 + tok0:G * 512 + tok0 + nt, :]
                    U.y_dst = lambda tok0, nt, b=b, G=G: yp[b, G * 512 + tok0:G * 512 + tok0 + nt, :]

                    def mk_out(dst, nh, dh, b=b, G=G):
                        def f(tok0, nt, sg):
                            p0 = G * 512 + tok0
                            return [(dst[b, :, p0:p0 + nt, :].rearrange("h s d -> s h d"),
                                     sg[0:nt, :].rearrange("p (h d) -> p h d", h=nh))]
                        return f
                    U.out_dk = mk_out(pdk, 4, 128)
                    U.out_dv = mk_out(pdv, 4, 128)
                    U.out_sk = mk_out(psk, 8, 64)
                    U.out_sv = mk_out(psv, 8, 64)

                    def diff_blocks(q0, W, h, G=G):
                        qb0 = (G * 512 + q0) // 128
                        nqb = W // 128
                        out = []
                        for kb in range(qb0 + nqb):
                            c0 = max(0, (kb - qb0) * 128)
                            out.append((kb, 128, kb * 128, c0, prompt_near(h, kb, qb0, nqb)))
                        return out
                    U.diff_blocks = diff_blocks

                    def sb_blocks(G=G):
                        qb0 = G * 4
                        out = []
                        for kb in range(qb0 + 3, -1, -1):
                            c0 = max(0, (kb - qb0) * 128)
                            diag = (128, HsbM, Jm) if kb >= qb0 else None
                            out.append((kb, 128, kb * 128, c0, diag))
                        return out
                    U.sb_blocks = sb_blocks
                    units.append((U, "p", b, G))

            def sample_pre():
                SC.op("pool", lambda e: e.memset(halo[:], 0.0), reads=["halo"], writes=["halo"])
                cache_jobs = []
                for h in range(4):
                    cache_jobs.append(("k", [(cdk[h], 0, 128)], kT[h], "kT%d" % h))
                for p in range(4):
                    cache_jobs.append(("k", [(csk[2 * p], 0, 64), (csk[2 * p + 1], 64, 64)], kT[4 + p], "kT%d" % (4 + p)))
                for h in range(4):
                    cache_jobs.append(("v", [(cdv[h], 0, 128)], (Vd, h * 128, 128), "Vd"))
                for hh in range(8):
                    cache_jobs.append(("v", [(csv[hh], 0, 64)], (Vs, hh * 64, 64), "Vs"))
                for ji, (kind, srcs, dst, dres) in enumerate(cache_jobs):
                    sl = ji % 2
                    cst_f = ar_f32(sl * 8, 2048).rearrange("p (a b) -> p a b", a=16)
                    cres = arr(sl * 8, 8)
                    wd_tot = sum(s[2] for s in srcs)
                    for (src, coff, wd) in srcs:
                        SC.dma(lambda e, src=src, coff=coff, wd=wd, cst_f=cst_f: e.dma_start(
                            out=cst_f[:, :, coff:coff + wd], in_=src.rearrange("(a p) d -> p a d", p=128)), "cst%d" % sl, writes=cres)
                    if kind == "v":
                        vt, vo, vw = dst
                        SC.op("pool" if ji % 2 else "dve", lambda e, vt=vt, vo=vo, vw=vw, cst_f=cst_f: e.tensor_copy(
                            out=vt[:, 0:16, vo:vo + vw], in_=cst_f[:, :, 0:vw]), reads=cres, writes=[dres])
                    else:
                        bres = arr(16, 4)
                        cbf = ar_bf(16, 2048).rearrange("p (a b) -> p a b", a=16)
                        SC.op("dve", lambda e, cbf=cbf, cst_f=cst_f: e.tensor_copy(out=cbf[:, :, :], in_=cst_f[:, :, :]), reads=cres, writes=bres)
                        for half in range(2):
                            pT = PS[half][:].bitcast(BF16).rearrange("p (a b) -> p a b", a=8)
                            for a in range(8):
                                SC.op("pe", lambda e, pT=pT, a=a, half=half, cbf=cbf: e.transpose(out=pT[:, a, :], in_=cbf[:, half * 8 + a, :], identity=ident),
                                      reads=bres + ["cstb"], writes=[PSn[half]])
                            SC.op("act" if half else "dve", (lambda e, pT=pT, half=half, dst=dst: e.activation(
                                out=dst[:, half * 1024:(half + 1) * 1024], in_=pT[:].rearrange("p a b -> p (a b)"), func=AF.Copy)) if half else
                                (lambda e, pT=pT, half=half, dst=dst: e.tensor_copy(out=dst[:, half * 1024:(half + 1) * 1024],
                                                                                   in_=pT[:].rearrange("p a b -> p (a b)"))),
                                reads=[PSn[half]], writes=[dres])
                for i in range(2):
                    SC.dma(lambda e, i=i: e.dma_start(out=halo[:, i, :], in_=dap(sconv, i * 2 * DFF, [[1, 128], [128, 44]]),
                                                     allow_slow_non_contiguous=True), "prm", reads=["halo"], writes=["halo"])

            U = Unit()
            U.T = DEC
            U.tiles = [(0, DEC)]
            U.kpos0 = PAST
            U.kb0 = 16
            U.x_src = lambda tok0, nt: xs[tok0:tok0 + nt, :]
            U.y_dst = lambda tok0, nt: ys[tok0:tok0 + nt, :]

            def mk_out_s(dst, nh):
                def f(tok0, nt, sg):
                    return [(dst[:, tok0:tok0 + nt, :].rearrange("h s d -> s h d"), sg[0:nt, :].rearrange("p (h d) -> p h d", h=nh))]
                return f
            U.out_dk = mk_out_s(sdk, 4)
            U.out_dv = mk_out_s(sdv, 4)
            U.out_sk = mk_out_s(ssk, 8)
            U.out_sv = mk_out_s(ssv, 8)

            def diff_blocks_s(q0, W, h):
                out = []
                for kb in range(16):
                    near = [(0, 16, Hb[:, h, 1, 0:16], Jm)] if kb == 15 else []
                    out.append((kb, 128, kb * 128, 0, near))
                out.append((16, 16, PAST, 0, [(0, 16, Hb16[:, h, :], J16)]))
                return out
            U.diff_blocks = diff_blocks_s

            def sb_blocks_s():
                out = [(16, 16, PAST, 0, (16, Hsb16, J16))]
                for kb in range(15, -1, -1):
                    out.append((kb, 128, kb * 128, 0, None))
                return out
            U.sb_blocks = sb_blocks_s
            units.append((U, "s", 0, 0))

            for ti in range(len(units[0][0].tiles)):
                phase1_tile(units[0][0], ti)
            for idx, (U, kind, b, G) in enumerate(units):
                nextU = units[idx + 1][0] if idx + 1 < len(units) else None
                if kind == "p":
                    if G == 0 and b > 0:
                        SC.op("pool", lambda e: e.memset(halo[:], 0.0), reads=["halo"], writes=["halo"])
                    run_unit(U, nextU)
                    if G == NU - 1:
                        conv_state_out(pcv[b])
                else:
                    chk('prompt')
                    sample_pre()
                    run_unit(U, None)
                    conv_state_out(scv)

        except _Stop:
            pass
        SC.finish()
    return nc


def _run(nc, in_maps):
    return run_bass_kernel_spmd(nc, in_maps, core_ids=list(range(len(in_maps))))


def kernel(x_prompt, x_sample, cache_diff_k, cache_diff_v, cache_sb_k, cache_sb_v, state_conv,
           attn_norm_w, w_in, lambda_q1, lambda_k1, lambda_q2, lambda_k2, diff_subln_w,
           sb_norm_w, w_out, ffn_norm_w, w_up, conv_w, conv_b, w_down, rel_bias, final_norm_w):
    NCORES = 8
    f = lambda a: np.ascontiguousarray(np.asarray(a, dtype=np.float32))
    B, S, _ = x_prompt.shape
    NSEQ = B // NCORES
    oh, cst = _constants()
    shared = {
        "w_in": f(w_in[0]), "w_out": f(w_out[0]), "w_up": f(w_up[0]), "w_down": f(w_down[0]),
        "anw": f(attn_norm_w[0]), "fnw": f(ffn_norm_w[0]), "finw": f(final_norm_w),
        "sublnw": f(diff_subln_w[0]), "sbw": f(sb_norm_w[0]),
        "lam4": f(np.stack([np.asarray(lambda_q1[0]), np.asarray(lambda_k1[0]), np.asarray(lambda_q2[0]), np.asarray(lambda_k2[0])])),
        "conv_w": f(conv_w[0]), "conv_b": f(conv_b[0]), "relb": f(rel_bias),
        "ohc": f(oh.reshape(32, 768)), "cstc": f(cst.reshape(128, 1408)),
    }
    xp = f(x_prompt)
    in_maps = []
    for c in range(NCORES):
        m = dict(shared)
        m["xp"] = xp[c * NSEQ:(c + 1) * NSEQ]
        m["xs"] = f(x_sample[c])
        m["cdk"] = f(cache_diff_k[0, c])
        m["cdv"] = f(cache_diff_v[0, c])
        m["csk"] = f(cache_sb_k[0, c])
        m["csv"] = f(cache_sb_v[0, c])
        m["sconv"] = f(state_conv[0, c])
        in_maps.append(m)
    nc = build_program(NSEQ, S)
    res = _run(nc, in_maps).results
    cat = lambda k: np.concatenate([r[k] for r in res], axis=0)
    stk = lambda k: np.stack([r[k] for r in res], axis=0)
    return (cat("yp"), stk("ys"),
            cat("pdk")[None], cat("pdv")[None], cat("psk")[None], cat("psv")[None], cat("pcv")[None],
            stk("sdk")[None], stk("sdv")[None], stk("ssk")[None], stk("ssv")[None], stk("scv")[None])
```

```python
import contextlib
import math
import numpy as np
import concourse.bass as bass
import concourse.mybir as mybir
from concourse.bass_utils import run_bass_kernel_spmd

F32 = mybir.dt.float32
BF16 = mybir.dt.bfloat16
AF = mybir.ActivationFunctionType
ALU = mybir.AluOpType

D = 1024
DFF = 2816
NJ = 22
PAST = 2048
DEC = 16
EPS = 1e-6
NEG = -30000.0
SEM_ROTATE = 30000
NWB = 3
ATTACH_WAITS = True


class Sched:
    ENGS = ("pe", "act", "dve", "pool", "sp")

    def __init__(self, nc, stack):
        self.nc = nc
        self.stack = stack
        self.eh = {"pe": nc.tensor, "act": nc.scalar, "dve": nc.vector, "pool": nc.gpsimd, "sp": nc.sync}
        self.sem = {}
        self.cnt = {}
        self.nsem = 0
        for e in self.ENGS:
            self._new_sem(e)
        self.waited = {e: {} for e in self.ENGS}
        self.lastw = {}
        self.readers = {}
        self.lanes = {}
        self.lane_of = {}

    def _new_sem(self, e):
        s = self.stack.enter_context(self.nc.semaphore("s_%s_%d" % (e, self.nsem)))
        self.nsem += 1
        self.sem[e] = s
        self.cnt[e] = 0

    def _deps(self, reads, writes):
        deps = []
        for r in reads:
            w = self.lastw.get(r)
            if w is not None:
                deps.append(w)
        for r in writes:
            w = self.lastw.get(r)
            if w is not None:
                deps.append(w)
            for x in self.readers.get(r, {}).values():
                deps.append(x)
        return deps

    def _emit_waits(self, eng, deps, skip_sem=None, attach_last=False):
        need = {}
        for (s, v) in deps:
            if skip_sem is not None and s is skip_sem:
                continue
            k = id(s)
            if k not in need or need[k][1] < v:
                need[k] = (s, v)
        wd = self.waited[eng]
        todo = []
        for k, (s, v) in need.items():
            L = self.lane_of.get(k)
            if L is not None and L[0] is s:
                v = L[1]
            if wd.get(k, 0) >= v:
                continue
            wd[k] = v
            todo.append((s, v))
        attach = None
        if attach_last and todo:
            attach = todo.pop()
        for (s, v) in todo:
            self.eh[eng].wait_ge(s, v)
        return attach

    def _mark(self, reads, writes, tok):
        for r in writes:
            self.lastw[r] = tok
            self.readers[r] = {}
        for r in reads:
            self.readers.setdefault(r, {})[id(tok[0])] = tok

    def op(self, eng, fn, reads=(), writes=()):
        deps = self._deps(reads, writes)
        skip = self.sem[eng] if eng == "pe" else None
        att = self._emit_waits(eng, deps, skip_sem=skip, attach_last=ATTACH_WAITS)
        if self.cnt[eng] >= SEM_ROTATE:
            self._new_sem(eng)
        self.cnt[eng] += 1
        tok = (self.sem[eng], self.cnt[eng])
        ins = fn(self.eh[eng])
        if att is not None:
            ins._wait_ge(att[0], att[1])
        ins.then_inc(tok[0], 1)
        self._mark(reads, writes, tok)

    def dma(self, fn, lane, reads=(), writes=(), q="sp"):
        if lane not in self.lanes:
            s = self.stack.enter_context(self.nc.semaphore("l_%s" % (lane,)))
            self.lanes[lane] = [s, 0]
            self.lane_of[id(s)] = self.lanes[lane]
        L = self.lanes[lane]
        deps = self._deps(reads, writes)
        self._emit_waits(q, deps)
        if L[1] >= SEM_ROTATE * 16:
            s = self.stack.enter_context(self.nc.semaphore("l_%s_%d" % (lane, self.nsem)))
            self.nsem += 1
            L[0] = s
            L[1] = 0
            self.lane_of[id(s)] = L
        L[1] += 16
        tok = (L[0], L[1])
        fn(self.eh[q]).then_inc(tok[0], 16)
        self._mark(reads, writes, tok)

    def finish(self, q="sp"):
        deps = [(L[0], L[1]) for L in self.lanes.values()]
        for e in self.ENGS:
            if e != q and self.cnt[e] > 0:
                deps.append((self.sem[e], self.cnt[e]))
        self._emit_waits(q, deps)


def _rel_bucket(rel):
    rel = np.asarray(rel, dtype=np.int64)
    half = 16
    max_exact = 8
    ret = np.where(rel > 0, half, 0)
    n = np.abs(rel)
    nf = np.maximum(n, 1).astype(np.float32)
    large = max_exact + (np.log(nf / np.float32(max_exact)) / np.float32(math.log(128 / max_exact))
                         * np.float32(half - max_exact)).astype(np.int32)
    large = np.minimum(large, half - 1)
    return ret + np.where(n < max_exact, n, large)


def _constants():
    oh = np.zeros((32, 3, 256), np.float32)
    i = np.arange(255)
    for t, d in enumerate((127 - i, -1 - i)):
        b = _rel_bucket(d)
        oh[b, t, i] = 1.0
        oh[15, t, i] -= 1.0
    i = np.arange(31)
    b = _rel_bucket(15 - i)
    oh[b, 2, i] = 1.0
    oh[15, 2, i] -= 1.0
    cst = np.zeros((128, 11, 128), np.float32)
    r = np.arange(128)
    cst[r, 0, r] = 1.0
    cst[r, 1, 127 - r] = 1.0
    cst[:, 2, :] = -1.0 * (r[:, None] >= r[None, :])
    cst[:, 3, :] = -1.0
    cst[:, 4, :] = 1.0
    cst[:, 5, :] = 1.0 / 128
    cst[:, 6, :] = ((r[:, None] // 64) == (r[None, :] // 64)) / 64.0
    cst[:, 7, :] = np.where((127 - r[:, None]) - r[None, :] >= 0, NEG, 0.0)
    cst[:, 8, :] = np.where((r[:, None] <= 63) & (r[None, :] < 64), NEG, 0.0)
    r16 = np.arange(16)
    cst[r16, 9, 15 - r16] = 1.0
    cst[0:16, 9, 16:32] = np.where((15 - r16[:, None]) - r16[None, :] >= 0, NEG, 0.0)
    return oh, cst


class _Stop(Exception):
    pass


def build_program(NSEQ, S, stop=None, dbg=False):
    nc = bass.Bass("TRN2", target_bir_lowering=False)

    def chk(tag):
        if stop == tag:
            raise _Stop()

    NU = S // 512
    NKB = S // 128

    def din(name, shape, dt=F32):
        return nc.dram_tensor(name, list(shape), dt, kind="ExternalInput").ap()

    def dout(name, shape):
        return nc.dram_tensor(name, list(shape), F32, kind="ExternalOutput").ap()

    xp = din("xp", [NSEQ, S, D])
    xs = din("xs", [DEC, D])
    cdk = din("cdk", [4, PAST, 128])
    cdv = din("cdv", [4, PAST, 128])
    csk = din("csk", [8, PAST, 64])
    csv = din("csv", [8, PAST, 64])
    sconv = din("sconv", [2, 2 * DFF])
    w_in = din("w_in", [D, 3072])
    w_out = din("w_out", [D, D])
    w_up = din("w_up", [D, 2 * DFF])
    w_down = din("w_down", [DFF, D])
    anw = din("anw", [D])
    fnw = din("fnw", [D])
    finw = din("finw", [D])
    sublnw = din("sublnw", [128])
    sbw = din("sbw", [64])
    lam4 = din("lam4", [4, 64])
    conv_w = din("conv_w", [3, 2 * DFF])
    conv_b = din("conv_b", [2 * DFF])
    relb = din("relb", [32, 4])
    ohc = din("ohc", [32, 768])
    cstc = din("cstc", [128, 1408])

    yp = dout("yp", [NSEQ, S, D])
    ys = dout("ys", [DEC, D])
    pdk = dout("pdk", [NSEQ, 4, S, 128])
    pdv = dout("pdv", [NSEQ, 4, S, 128])
    psk = dout("psk", [NSEQ, 8, S, 64])
    psv = dout("psv", [NSEQ, 8, S, 64])
    pcv = dout("pcv", [NSEQ, 2, 2 * DFF])
    sdk = dout("sdk", [4, DEC, 128])
    sdv = dout("sdv", [4, DEC, 128])
    ssk = dout("ssk", [8, DEC, 64])
    ssv = dout("ssv", [8, DEC, 64])
    scv = dout("scv", [2, 2 * DFF])

    if dbg:
        d_mix = nc.dram_tensor("d_mix", [128, 4096], BF16, kind="ExternalOutput").ap()
        d_x1 = nc.dram_tensor("d_x1", [512, 1024], F32, kind="ExternalOutput").ap()
        d_gT = nc.dram_tensor("d_gT", [128, NJ * 512], BF16, kind="ExternalOutput").ap()
        d_hT = nc.dram_tensor("d_hT", [128, 4096], BF16, kind="ExternalOutput").ap()
    dbg_done = [False]
    wsc = nc.dram_tensor("wsc", [25, 128, 4096], BF16, kind="Internal").ap()
    gsc = nc.dram_tensor("gsc", [4, 768], F32, kind="Internal").ap()

    def dap(t, offset, pat):
        return bass.AP(tensor=t.tensor, offset=offset, ap=[list(p) for p in pat])

    with contextlib.ExitStack() as st:
        E = st.enter_context
        SC = Sched(nc, st)
        try:

            def sb(name, shape, dt):
                return E(nc.sbuf_tensor(name, list(shape), dt))

            kT = [sb("kT%d" % i, [128, PAST + DEC], BF16) for i in range(8)]
            Vd = sb("Vd", [128, 17, 512], BF16)
            Vs = sb("Vs", [128, 17, 512], BF16)
            qz = [sb("qz%d" % i, [128, 512], BF16) for i in range(16)]
            hT = sb("hT", [128, 8, 512], BF16)
            mixT = sb("mixT", [128, 8, 512], BF16)
            gT = sb("gT", [128, NJ, 512], BF16)
            xt = [sb("xt%d" % i, [128, D], F32) for i in range(4)]
            wb = [sb("wb%d" % i, [128, 8, 512], BF16) for i in range(NWB)]
            hb = [sb("hb%d" % i, [128, D], BF16) for i in range(2)]
            stg = [sb("stg%d" % i, [128, 512], F32) for i in range(2)]
            fin_b = sb("fin_b", [128, D], F32)
            xs_stage = sb("xs_stage", [128, D], F32)
            ctmp = sb("ctmp", [128, 16], F32)
            Hb = sb("Hb", [128, 4, 2, 128], BF16)
            Hb16 = sb("Hb16", [128, 4, 16], BF16)
            cstb = sb("cstb", [128, 11, 128], BF16)
            identf = sb("identf", [128, 128], F32)
            cw = sb("cw", [128, 3, 44], F32)
            cb = sb("cb", [128, 44], F32)
            halo = sb("halo", [128, 2, 44], F32)
            anw_c = sb("anw_c", [128, 8], F32)
            fnw_c = sb("fnw_c", [128, 8], F32)
            small = sb("small", [128, 32], F32)
            lamb = sb("lamb", [128, 4, 64], F32)
            cbias = sb("cbias", [128, 4], F32)
            stat = sb("stat", [128, 16], F32)
            AR = sb("AR", [128, 21 * 256], F32)

            def ar_f32(slot, n):
                return AR[:, slot * 256: slot * 256 + n]

            def ar_bf(slot, n):
                return AR[:, slot * 256: slot * 256 + (n + 1) // 2].bitcast(BF16)

            def arr(slot, nslots):
                return ["ar%d" % (slot + i) for i in range(nslots)]

            PSA = E(nc.psum_tensor("psa", [128, 4096], F32))
            PS = [PSA[:, i * 512:(i + 1) * 512] for i in range(8)]
            PSn = ["ps%d" % i for i in range(8)]

            ident = cstb[:, 0, :]
            Jm = cstb[:, 1, :]
            negA = cstb[:, 2, :]
            negOnes = cstb[:, 3, :]
            onesb = cstb[:, 4, :]
            mean128 = cstb[:, 5, :]
            mean64 = cstb[:, 6, :]
            HsbM = cstb[:, 7, :]
            zerosb = cstb[:, 10, :]
            J16 = cstb[:, 9, 0:16]
            Hsb16 = cstb[:, 9, 16:32]

            SC.dma(lambda e: e.dma_start(out=xt[0][:, 0:1024], in_=cstc[:, 0:1024]), "xt0", writes=["xt0"])
            SC.dma(lambda e: e.dma_start(out=xt[1][:, 0:384], in_=cstc[:, 1024:1408]), "xt1", writes=["xt1"])
            SC.op("dve", lambda e: e.tensor_copy(out=cstb[:, 0:8, :], in_=xt[0][:, 0:1024].rearrange("p (a b) -> p a b", a=8)),
                  reads=["xt0"], writes=["cstb"])
            SC.op("dve", lambda e: e.tensor_copy(out=cstb[:, 8:11, :], in_=xt[1][:, 0:384].rearrange("p (a b) -> p a b", a=3)),
                  reads=["xt1"], writes=["cstb"])
            SC.op("dve", lambda e: e.tensor_copy(out=identf[:], in_=xt[0][:, 0:128]), reads=["xt0"], writes=["identf"])
            SC.dma(lambda e: e.dma_start(out=anw_c[:], in_=dap(anw, 0, [[1, 128], [128, 8]]), allow_slow_non_contiguous=True),
                   "prm", writes=["anw_c"])
            SC.dma(lambda e: e.dma_start(out=fnw_c[:], in_=dap(fnw, 0, [[1, 128], [128, 8]]), allow_slow_non_contiguous=True),
                   "prm", writes=["fnw_c"])
            for i in range(3):
                SC.dma(lambda e, i=i: e.dma_start(out=cw[:, i, :], in_=dap(conv_w, i * 2 * DFF, [[1, 128], [128, 44]]),
                                                 allow_slow_non_contiguous=True), "prm", writes=["cw"])
            SC.dma(lambda e: e.dma_start(out=cb[:], in_=dap(conv_b, 0, [[1, 128], [128, 44]]), allow_slow_non_contiguous=True),
                   "prm", writes=["cb"])
            SC.dma(lambda e: e.dma_start(out=small[:, 0:1], in_=dap(sublnw, 0, [[1, 128], [1, 1]])), "prm", writes=["small"])
            SC.dma(lambda e: e.dma_start(out=small[0:64, 1:2], in_=dap(sbw, 0, [[1, 64], [1, 1]])), "prm", writes=["small"])
            SC.dma(lambda e: e.dma_start(out=small[64:128, 1:2], in_=dap(sbw, 0, [[1, 64], [1, 1]])), "prm", writes=["small"])
            SC.dma(lambda e: e.dma_start(out=fin_b[:], in_=dap(finw, 0, [[0, 128], [1, D]])), "prm", writes=["fin_b"])
            SC.dma(lambda e: e.dma_start(out=lamb[:].rearrange("p a b -> p (a b)"), in_=dap(lam4, 0, [[0, 128], [1, 256]])),
                   "prm", writes=["lamb"])
            SC.dma(lambda e: e.dma_start(out=cbias[:], in_=dap(relb, 15 * 4, [[0, 128], [1, 4]])), "prm", writes=["cbias"])
            SC.op("pool", lambda e: e.memset(small[:, 2:3], EPS), writes=["small_eps"])
            SC.op("pool", lambda e: e.memset(halo[:], 0.0), writes=["halo"])
            for i in range(16):
                SC.op("pool", lambda e, i=i: e.memset(qz[i][:], 0.0), writes=["qz%d" % i])
            SC.op("dve", lambda e: e.tensor_tensor(out=lamb[:, 0, :], in0=lamb[:, 0, :], in1=lamb[:, 1, :], op=ALU.mult),
                  reads=["lamb"], writes=["lamb"])
            SC.op("dve", lambda e: e.tensor_tensor(out=lamb[:, 2, :], in0=lamb[:, 2, :], in1=lamb[:, 3, :], op=ALU.mult),
                  reads=["lamb"], writes=["lamb"])
            SC.op("dve", lambda e: e.tensor_scalar(out=lamb[:, 1, :], in0=lamb[:, 0, :], scalar1=1.0, scalar2=0.0, op0=ALU.mult,
                                                   op1=ALU.add, accum_out=small[:, 3:4]), reads=["lamb"], writes=["lamb", "small"])
            SC.op("dve", lambda e: e.tensor_scalar(out=lamb[:, 3, :], in0=lamb[:, 2, :], scalar1=1.0, scalar2=0.0, op0=ALU.mult,
                                                   op1=ALU.add, accum_out=small[:, 4:5]), reads=["lamb"], writes=["lamb", "small"])
            SC.op("act", lambda e: e.activation(out=small[:, 5:7], in_=small[:, 3:5], func=AF.Exp), reads=["small"], writes=["small"])
            SC.op("dve", lambda e: e.scalar_tensor_tensor(out=small[:, 7:8], in0=small[:, 6:7], scalar=-0.2, in1=small[:, 5:6],
                                                          op0=ALU.add, op1=ALU.subtract), reads=["small"], writes=["small"])
            SC.op("dve", lambda e: e.tensor_scalar_mul(out=small[:, 8:9], in0=small[:, 0:1], scalar1=0.8),
                  reads=["small"], writes=["small"])
            neglam = small[:, 7:8]
            subw8 = small[:, 8:9]
            sbwc = small[:, 1:2]
            epsc = small[:, 2:3]

            chk('consts')
            SC.dma(lambda e: e.dma_start(out=xt[2][0:32, 0:768], in_=ohc[:, :]), "xt2", writes=["xt2"])
            SC.dma(lambda e: e.dma_start(out=xt[2][0:32, 768:772], in_=relb[:, :]), "xt2", writes=["xt2"])
            SC.op("pe", lambda e: e.matmul(PS[0][0:4, 0:512], lhsT=xt[2][0:32, 768:772], rhs=xt[2][0:32, 0:512], start=True, stop=True),
                  reads=["xt2"], writes=["ps0"])
            SC.op("pe", lambda e: e.matmul(PS[1][0:4, 0:256], lhsT=xt[2][0:32, 768:772], rhs=xt[2][0:32, 512:768], start=True, stop=True),
                  reads=["xt2"], writes=["ps1"])
            SC.op("dve", lambda e: e.tensor_copy(out=xt[3][0:4, 0:512], in_=PS[0][0:4, 0:512]), reads=["ps0"], writes=["xt3"])
            SC.op("dve", lambda e: e.tensor_copy(out=xt[3][0:4, 512:768], in_=PS[1][0:4, 0:256]), reads=["ps1"], writes=["xt3"])
            SC.dma(lambda e: e.dma_start(out=gsc[:, :], in_=xt[3][0:4, 0:768]), "xt3", reads=["xt3"], writes=["gsc"])
            for h in range(4):
                for t in range(2):
                    SC.dma(lambda e, h=h, t=t: e.dma_start(out=xt[2][:, (h * 2 + t) * 128:(h * 2 + t + 1) * 128],
                                                           in_=dap(gsc, h * 768 + t * 256, [[1, 128], [1, 128]])),
                           "xt2", reads=["gsc"], writes=["xt2"])
                SC.dma(lambda e, h=h: e.dma_start(out=xt[3][0:16, 800 + h * 16: 800 + (h + 1) * 16],
                                                  in_=dap(gsc, h * 768 + 512, [[1, 16], [1, 16]])),
                       "xt3", reads=["gsc"], writes=["xt3"])
            for h in range(4):
                SC.op("dve", lambda e, h=h: e.tensor_tensor(out=Hb[:, h, 0, :], in0=xt[2][:, (h * 2) * 128:(h * 2 + 1) * 128],
                                                            in1=xt[1][:, 0:128], op=ALU.add), reads=["xt2", "xt1"], writes=["Hb"])
                SC.op("dve", lambda e, h=h: e.tensor_copy(out=Hb[:, h, 1, :], in_=xt[2][:, (h * 2 + 1) * 128:(h * 2 + 2) * 128]),
                      reads=["xt2"], writes=["Hb"])
            SC.op("pool", lambda e: e.memset(Hb16[:], 0.0), writes=["Hb16"])
            SC.op("dve", lambda e: e.tensor_copy(out=Hb16[0:16].rearrange("p a b -> p (a b)"), in_=xt[3][0:16, 800:864]),
                  reads=["xt3", "Hb16"], writes=["Hb16"])

            chk('bias')
            def piece_src(i):
                if i < 6:
                    return [(dap(w_in, i * 512, [[3072, 128], [128 * 3072, 8], [1, 512]]), None)], anw_c, 8
                if i < 8:
                    return [(dap(w_out, (i - 6) * 512, [[D, 128], [128 * D, 8], [1, 512]]), None)], None, 8
                if i < 19:
                    k = i - 8
                    return [(dap(w_up, k * 256, [[2 * DFF, 128], [128 * 2 * DFF, 8], [1, 256]]), (0, 256)),
                            (dap(w_up, DFF + k * 256, [[2 * DFF, 128], [128 * 2 * DFF, 8], [1, 256]]), (256, 512))], fnw_c, 8
                k = i - 19
                hf, jp = k // 3, k % 3
                nj = 8 if jp < 2 else 6
                return [(dap(w_down, (jp * 8 * 128) * D + hf * 512, [[D, 128], [128 * D, nj], [1, 512]]), None)], None, nj

            cast_engs = ["dve", "act"]
            ce = 0
            for i in range(25):
                srcs, scol, nj = piece_src(i)
                sslot = i % 2
                if sslot == 0:
                    views = [xt[c // 2][:, (c % 2) * 512:(c % 2) * 512 + 512] for c in range(8)]
                    alias = ["xt%d" % (c // 2) for c in range(8)]
                else:
                    gflat = gT[:].rearrange("p a b -> p (a b)")
                    views = [gflat[:, c * 1024:(c + 1) * 1024].bitcast(F32) for c in range(8)]
                    alias = ["gTa"] * 8
                for si, (src, cols) in enumerate(srcs):
                    a, b_ = (0, 512) if cols is None else cols
                    groups = [list(range(g, min(g + 2, nj))) for g in range(0, nj, 2)] if sslot == 0 else [list(range(nj))]
                    for grp in groups:
                        wr = ["wstg%d_%d_%d" % (sslot, c, si) for c in grp]
                        if i < 2:
                            wr += [alias[c] for c in grp]
                        c_lo, c_hi = grp[0], grp[-1] + 1
                        if sslot == 0:
                            dstv = xt[c_lo // 2][:, :].rearrange("p (a b) -> p a b", a=2)[:, 0:c_hi - c_lo, a:b_]
                        else:
                            dstv = gT[:].rearrange("p a b -> p (a b)")[:, 0:8192].bitcast(F32).rearrange("p (a b) -> p a b", a=8)[:, c_lo:c_hi, a:b_]
                        SC.dma(lambda e, src=src, c_lo=c_lo, c_hi=c_hi, dstv=dstv: e.dma_start(out=dstv, in_=src[:, c_lo:c_hi, :]),
                               "wst%d" % sslot, writes=wr)
                wslot = i % NWB
                for c in range(nj):
                    eng = cast_engs[ce % 2]
                    ce += 1
                    rd = ["wstg%d_%d_%d" % (sslot, c, si) for si in range(len(srcs))] + [alias[c]]
                    if scol is None:
                        if eng == "act":
                            SC.op("act", lambda e, c=c, v=views, w=wslot: e.activation(out=wb[w][:, c, :], in_=v[c][:, :], func=AF.Copy),
                                  reads=rd, writes=["wbc%d_%d" % (wslot, c)])
                        else:
                            SC.op(eng, lambda e, c=c, v=views, w=wslot: e.tensor_copy(out=wb[w][:, c, :], in_=v[c][:, :]),
                                  reads=rd, writes=["wbc%d_%d" % (wslot, c)])
                    else:
                        if eng == "act":
                            SC.op("act", lambda e, c=c, v=views, w=wslot, scol=scol: e.activation(
                                out=wb[w][:, c, :], in_=v[c][:, :], func=AF.Copy, scale=scol[:, c:c + 1]),
                                reads=rd + ["anw_c", "fnw_c"], writes=["wbc%d_%d" % (wslot, c)])
                        else:
                            SC.op(eng, lambda e, c=c, v=views, w=wslot, scol=scol: e.tensor_scalar_mul(
                                out=wb[w][:, c, :], in0=v[c][:, :], scalar1=scol[:, c:c + 1]),
                                reads=rd + ["anw_c", "fnw_c"], writes=["wbc%d_%d" % (wslot, c)])
                if nj < 8:
                    SC.op("pool", lambda e, w=wslot: e.memset(wb[w][:, nj:8, :], 0.0), writes=["wbc%d_%d" % (wslot, c_) for c_ in range(nj, 8)])
                SC.dma(lambda e, i=i, w=wslot: e.dma_start(out=wsc[i, :, :], in_=wb[w][:].rearrange("p a b -> p (a b)")),
                       "wb%d" % wslot, reads=["wb%d" % wslot] + ["wbc%d_%d" % (wslot, c_) for c_ in range(8)], writes=["wsc%d" % i], q="act")

            chk('wcast')
            wstate = {"next_load": 0, "next_use": 0, "total": 0}
            seq_pieces = []

            def wload_next():
                k = wstate["next_load"]
                if k >= len(seq_pieces):
                    return
                i = seq_pieces[k]
                w = k % NWB
                SC.dma(lambda e, i=i, w=w: e.dma_start(out=wb[w][:].rearrange("p a b -> p (a b)"), in_=wsc[i, :, :]),
                       "wb%d" % w, reads=["wsc%d" % i], writes=["wb%d" % w])
                wstate["next_load"] += 1

            def wget(i):
                k = wstate["next_use"]
                assert seq_pieces[k] == i, (k, i, seq_pieces[k])
                wstate["next_use"] += 1
                return k % NWB

            def wdone():
                wload_next()

            evac_rr = [0]

            def rms_rstd(xtile, nt, xres, col, junk, jres):
                SC.op("act", lambda e: e.activation(out=junk[0:nt, 0:512], in_=xtile[0:nt, 0:512], func=AF.Square,
                                                    accum_out=stat[0:nt, col:col + 1]), reads=[xres], writes=[jres, "stat%d" % col])
                SC.op("act", lambda e: e.activation(out=junk[0:nt, 512:1024], in_=xtile[0:nt, 512:1024], func=AF.Square,
                                                    accum_out=stat[0:nt, col + 1:col + 2]), reads=[xres], writes=[jres, "stat%d" % (col + 1)])
                SC.op("dve", lambda e: e.tensor_tensor(out=stat[0:nt, col:col + 1], in0=stat[0:nt, col:col + 1],
                                                       in1=stat[0:nt, col + 1:col + 2], op=ALU.add),
                      reads=["stat%d" % col, "stat%d" % (col + 1)], writes=["stat%d" % col])
                SC.op("act", lambda e: e.activation(out=stat[0:nt, col:col + 1], in_=stat[0:nt, col:col + 1], func=AF.Ln,
                                                    scale=1.0 / D, bias=epsc[0:nt, :]), reads=["stat%d" % col, "small_eps"], writes=["stat%d" % col])
                SC.op("act", lambda e: e.activation(out=stat[0:nt, col:col + 1], in_=stat[0:nt, col:col + 1], func=AF.Exp, scale=-0.5),
                      reads=["stat%d" % col], writes=["stat%d" % col])

            def norm_A(xtile, xres, nt, hbi, col):
                hbt = hb[hbi]
                hres = "hb%d" % hbi
                rms_rstd(xtile, nt, xres, col, hbt, hres)
                SC.op("dve", lambda e: e.tensor_scalar_mul(out=hbt[0:nt, :], in0=xtile[0:nt, :], scalar1=stat[0:nt, col:col + 1]), reads=[xres, "stat%d" % col], writes=[hres])

            def norm_B(tok0, nt, hbi, dstT, dres):
                hbt = hb[hbi]
                hres = "hb%d" % hbi
                pT = PS[7][:].bitcast(BF16).rearrange("p (a b) -> p a b", a=8)
                for c in range(8):
                    SC.op("pe", lambda e, c=c: e.transpose(out=pT[:, c, 0:nt], in_=hbt[0:nt, c * 128:(c + 1) * 128],
                                                           identity=cstb[0:nt, 0, 0:nt]),
                          reads=[hres, "cstb"], writes=["ps7"])
                SC.op("dve", lambda e: e.tensor_copy(out=dstT[:, :, tok0:tok0 + nt], in_=pT[:, :, 0:nt]), reads=["ps7"], writes=[dres])

            def norm_transpose(xtile, xres, tok0, nt, hbi, col, dstT, dres):
                norm_A(xtile, xres, nt, hbi, col)
                norm_B(tok0, nt, hbi, dstT, dres)

            def phase1_A(U, ti):
                tok0, nt = U.tiles[ti]
                SC.dma(lambda e: e.dma_start(out=xs_stage[0:nt, :], in_=U.x_src(tok0, nt)), "xs_stage", writes=["xs_stage"])
                norm_A(xs_stage, "xs_stage", nt, ti % 2, 2 * ti)

            def phase1_B(U, ti):
                tok0, nt = U.tiles[ti]
                norm_B(tok0, nt, ti % 2, mixT, "mixT")

            def phase1_tile(U, ti):
                phase1_A(U, ti)
                phase1_B(U, ti)

            class Unit:
                pass

            def run_unit(U, nextU):
                T = U.T
                tiles = U.tiles
                for ti, (tok0, nt) in enumerate(tiles):
                    SC.dma(lambda e, ti=ti, tok0=tok0, nt=nt: e.dma_start(out=xt[ti][0:nt, :], in_=U.x_src(tok0, nt)),
                           "xt%d" % ti, writes=["xt%d" % ti])
                hsrc = mixT

                chk('p1')
                def feat_major(w, dst_list, kcol, scale):
                    for gi in range(4):
                        bank = evac_rr[0] % 2
                        evac_rr[0] += 1
                        for c in range(8):
                            SC.op("pe", lambda e, c=c, gi=gi, bank=bank: e.matmul(
                                PS[bank][:, 0:T], lhsT=wb[w][:, c, gi * 128:(gi + 1) * 128], rhs=hsrc[:, c, 0:T],
                                start=(c == 0), stop=(c == 7)), reads=["wb%d" % w, "mixT"], writes=[PSn[bank]])
                        for di, (dst, dres, p0, p1) in enumerate(dst_list[gi]):
                            if (bank + di) % 2 == 0:
                                SC.op("act", lambda e, dst=dst, bank=bank, p0=p0, p1=p1: e.activation(
                                    out=dst[p0:p1, kcol:kcol + T], in_=PS[bank][p0:p1, 0:T], func=AF.Copy, scale=scale),
                                    reads=[PSn[bank]], writes=[dres])
                            else:
                                SC.op("dve", lambda e, dst=dst, bank=bank, p0=p0, p1=p1: e.tensor_scalar_mul(
                                    out=dst[p0:p1, kcol:kcol + T], in0=PS[bank][p0:p1, 0:T], scalar1=scale),
                                    reads=[PSn[bank]], writes=[dres])

                def tok_major(w, out_fn, vdst):
                    for ti, (tok0, nt) in enumerate(tiles):
                        bank = evac_rr[0] % 2
                        evac_rr[0] += 1
                        for c in range(8):
                            SC.op("pe", lambda e, c=c, bank=bank, tok0=tok0, nt=nt: e.matmul(
                                PS[bank][0:nt, 0:512], lhsT=hsrc[:, c, tok0:tok0 + nt], rhs=wb[w][:, c, :],
                                start=(c == 0), stop=(c == 7)), reads=["wb%d" % w, "mixT"], writes=[PSn[bank]])
                        sg = stg[bank]
                        SC.op("act", lambda e, bank=bank, nt=nt, sg=sg: e.activation(out=sg[0:nt, :], in_=PS[bank][0:nt, 0:512], func=AF.Copy),
                              reads=[PSn[bank]], writes=["stg%d" % bank])
                        if vdst is not None:
                            vt, vres = vdst
                            kb = U.kb0 + ti
                            SC.op("pool", lambda e, nt=nt, sg=sg, kb=kb, vt=vt: e.tensor_copy(out=vt[0:nt, kb, :], in_=sg[0:nt, :]),
                                  reads=["stg%d" % bank], writes=[vres])
                        for (dst, src) in out_fn(tok0, nt, sg):
                            SC.dma(lambda e, dst=dst, src=src: e.dma_start(out=dst, in_=src), "o_stg%d" % bank, reads=["stg%d" % bank], q="pool")

                w = wget(0)
                feat_major(w, [[(qz[2 * h], "qz%d" % (2 * h), 0, 64), (qz[2 * h + 1], "qz%d" % (2 * h + 1), 64, 128)] for h in range(4)], 0, 0.125)
                wdone()
                w = wget(1)
                feat_major(w, [[(kT[h], "kT%d" % h, 0, 128)] for h in range(4)], U.kpos0, 1.0)
                tok_major(w, U.out_dk, None)
                wdone()
                w = wget(2)
                tok_major(w, U.out_dv, (Vd, "Vd"))
                wdone()
                w = wget(3)
                feat_major(w, [[(qz[8 + 2 * p], "qz%d" % (8 + 2 * p), 0, 64), (qz[9 + 2 * p], "qz%d" % (9 + 2 * p), 64, 128)] for p in range(4)], 0, 0.125)
                wdone()
                w = wget(4)
                feat_major(w, [[(kT[4 + p], "kT%d" % (4 + p), 0, 128)] for p in range(4)], U.kpos0, 1.0)
                tok_major(w, U.out_sk, None)
                wdone()
                w = wget(5)
                tok_major(w, U.out_sv, (Vs, "Vs"))
                wdone()

                chk('p2')
                W = min(256, T)
                nsub = T // W
                Sbk = [0, 1, 4, 5]
                Sv = [PS[b_][:, 0:2 * W].rearrange("p (a b) -> p a b", a=2) for b_ in Sbk]
                Sbanks = [[PSn[b_]] for b_ in Sbk]
                NSB = 3 if T >= 256 else 2
                ODbanks = [(2, 3), (6, 7)]
                pend = [None]

                def diff_job(ji, h, u):
                    q0 = u * W
                    blocks = U.diff_blocks(q0, W, h)
                    ob, db = ODbanks[ji % 2]
                    Ov = PS[ob][:, 0:2 * W].rearrange("p (a b) -> p a b", a=2)
                    Dv = PS[db][:, 0:2 * W].rearrange("p (a b) -> p a b", a=2)
                    nb = len(blocks)

                    def stage1(bi):
                        kb, ksz, kcol, c0, near = blocks[bi]
                        sbk = bi % NSB
                        pbk = [0, 2, 12][bi % NSB]
                        Sb = Sv[sbk]
                        SC.op("pe", lambda e: e.matmul(
                            Sb[0:ksz, 0, c0:W], lhsT=kT[h][:, kcol:kcol + ksz], rhs=qz[2 * h][:, q0 + c0:q0 + W],
                            start=True, stop=False, skip_group_check=True),
                            reads=["kT%d" % h, "qz%d" % (2 * h)], writes=Sbanks[sbk])
                        SC.op("pe", lambda e: e.matmul(
                            Sb[0:ksz, 1, c0:W], lhsT=kT[h][:, kcol:kcol + ksz], rhs=qz[2 * h + 1][:, q0 + c0:q0 + W],
                            start=False, stop=False, skip_group_check=True),
                            reads=["kT%d" % h, "qz%d" % (2 * h + 1)], writes=Sbanks[sbk])
                        for (coff, wd, Ht, Jt) in near:
                            for half in range(2):
                                SC.op("pe", lambda e, coff=coff, wd=wd, Ht=Ht, Jt=Jt, half=half: e.matmul(
                                    Sb[0:ksz, half, coff:coff + wd], lhsT=Jt, rhs=Ht, start=False, stop=False, skip_group_check=True),
                                    reads=["Hb", "Hb16", "cstb"], writes=Sbanks[sbk])
                        Pt = ar_bf(pbk, 2 * W).rearrange("p (a b) -> p a b", a=2)
                        SC.op("act", lambda e: e.activation(
                            out=Pt[0:ksz, :, c0:W], in_=Sb[0:ksz, :, c0:W], func=AF.Exp, bias=cbias[0:ksz, h:h + 1]),
                            reads=Sbanks[sbk] + ["cbias"], writes=arr(pbk, 2))

                    def stage2(bi):
                        kb, ksz, kcol, c0, near = blocks[bi]
                        pbk = [0, 2, 12][bi % NSB]
                        Pt = ar_bf(pbk, 2 * W).rearrange("p (a b) -> p a b", a=2)
                        pres = arr(pbk, 2)
                        first = (bi == 0)
                        for half in range(2):
                            SC.op("pe", lambda e, half=half: e.matmul(
                                Ov[:, half, c0:W], lhsT=Vd[0:ksz, kb, h * 128:(h + 1) * 128], rhs=Pt[0:ksz, half, c0:W],
                                start=(first and half == 0), stop=False, skip_group_check=True),
                                reads=pres + ["Vd"], writes=[PSn[ob]])
                        for half in range(2):
                            SC.op("pe", lambda e, half=half: e.matmul(
                                Dv[:, half, c0:W], lhsT=onesb[0:ksz, :], rhs=Pt[0:ksz, half, c0:W],
                                start=(first and half == 0), stop=False, skip_group_check=True),
                                reads=pres + ["cstb"], writes=[PSn[db]])

                    stage1(0)
                    if nb > 1 and NSB == 3:
                        stage1(1)
                    for bi in range(nb):
                        nxt = bi + 2 if NSB == 3 else bi + 1
                        if nxt < nb:
                            stage1(nxt)
                        stage2(bi)
                        if pend[0] is not None and (bi == 1 or bi == nb - 1):
                            pend[0]()
                            pend[0] = None
                    rD = ar_f32(4, 2 * W).rearrange("p (a b) -> p a b", a=2)
                    on = ar_f32(6, 2 * W).rearrange("p (a b) -> p a b", a=2)
                    od = ar_f32(8, W)
                    sq = ar_bf(9, W)
                    rr = ar_f32(10, W)
                    SC.op("dve", lambda e: e.reciprocal(out=rD[:, :, :], in_=Dv[:, :, :]), reads=[PSn[db]], writes=arr(4, 2))
                    SC.op("dve", lambda e: e.tensor_tensor(out=on[:, :, :], in0=Ov[:, :, :], in1=rD[:, :, :], op=ALU.mult),
                          reads=[PSn[ob]] + arr(4, 2), writes=arr(6, 2))
                    SC.op("dve", lambda e: e.scalar_tensor_tensor(out=od[:, :], in0=on[:, 1, :], scalar=neglam,
                                                                  in1=on[:, 0, :], op0=ALU.mult, op1=ALU.add),
                          reads=arr(6, 2) + ["small"], writes=arr(8, 1))
                    SC.op("pool", lambda e: e.tensor_tensor(out=sq[:, :], in0=od[:, :], in1=od[:, :], op=ALU.mult),
                          reads=arr(8, 1), writes=arr(9, 1))

                    def epiB():
                        SC.op("pe", lambda e: e.matmul(PS[db][:, 0:W], lhsT=mean128, rhs=sq[:, :], start=True, stop=True),
                              reads=arr(9, 1) + ["cstb"], writes=[PSn[db]])
                        SC.op("act", lambda e: e.activation(out=rr[:, :], in_=PS[db][:, 0:W], func=AF.Ln, bias=epsc),
                              reads=[PSn[db], "small_eps"], writes=arr(10, 1))
                        SC.op("act", lambda e: e.activation(out=rr[:, :], in_=rr[:, :], func=AF.Exp, scale=-0.5),
                              reads=arr(10, 1), writes=arr(10, 1))
                        SC.op("dve", lambda e: e.scalar_tensor_tensor(
                            out=mixT[:, h, q0:q0 + W], in0=od[:, :], scalar=subw8, in1=rr[:, :], op0=ALU.mult, op1=ALU.mult),
                            reads=arr(8, 1) + arr(10, 1) + ["small"], writes=["mixT"])
                    pend[0] = epiB

                ji = 0
                for h in range(4):
                    for u in range(nsub):
                        diff_job(ji, h, u)
                        ji += 1
                if pend[0] is not None:
                    pend[0]()
                    pend[0] = None

                chk('p3a')
                Lsum = ar_bf(20, 512)
                for p in range(4):
                    for ehd in range(2):
                        hh = 2 * p + ehd
                        pr0 = ehd * 64
                        blocks = U.sb_blocks()
                        SC.op("pool", lambda e: e.memset(Lsum[:, :], 0.0), writes=arr(20, 1))
                        nb = len(blocks)

                        def s1(bi):
                            kb, ksz, kcol, c0, diag = blocks[bi]
                            zb = bi % 2
                            eb = [0, 2, 18][bi % 3]
                            SC.op("pe", lambda e: e.matmul(PS[zb][0:ksz, c0:T], lhsT=kT[4 + p][:, kcol:kcol + ksz],
                                                           rhs=qz[8 + hh][:, c0:T], start=True, stop=False, skip_group_check=True),
                                  reads=["kT%d" % (4 + p), "qz%d" % (8 + hh)], writes=[PSn[zb]])
                            if diag is not None:
                                dw, Ht, Jt = diag
                                SC.op("pe", lambda e: e.matmul(PS[zb][0:ksz, c0:c0 + dw], lhsT=Jt, rhs=Ht, start=False, stop=False,
                                                               skip_group_check=True), reads=["cstb"], writes=[PSn[zb]])
                            Et = ar_f32(eb, 512)
                            Lp = ar_bf(4 + zb, 512)
                            SC.op("act", lambda e: e.activation(out=Et[0:ksz, c0:T], in_=PS[zb][0:ksz, c0:T], func=AF.Exp),
                                  reads=[PSn[zb]], writes=arr(eb, 2))
                            SC.op("act", lambda e: e.activation(out=Lp[0:ksz, c0:T], in_=Et[0:ksz, c0:T], func=AF.Ln, bias=1.0),
                                  reads=arr(eb, 2), writes=arr(4 + zb, 1))

                        def s2a(bi):
                            kb, ksz, kcol, c0, diag = blocks[bi]
                            zb = bi % 2
                            eb = [0, 2, 18][bi % 3]
                            cbk = 2 + zb
                            Et = ar_f32(eb, 512)
                            Lp = ar_bf(4 + zb, 512)
                            Wt = ar_bf(6 + zb, 512)
                            Xt = ar_f32(14 + 2 * zb, 512)
                            SC.op("pe", lambda e: e.matmul(PS[cbk][0:ksz, c0:T], lhsT=negA[0:ksz, 0:ksz], rhs=Lp[0:ksz, c0:T],
                                                           start=True, stop=False, skip_group_check=True),
                                  reads=arr(4 + zb, 1) + ["cstb"], writes=[PSn[cbk]])
                            if diag is not None:
                                dw, Ht, Jt = diag
                                SC.op("pe", lambda e: e.matmul(PS[cbk][0:ksz, c0:c0 + dw], lhsT=Jt, rhs=Ht, start=False, stop=False,
                                                               skip_group_check=True), reads=["cstb"], writes=[PSn[cbk]])
                            if bi > 0:
                                SC.op("pe", lambda e: e.matmul(PS[cbk][0:ksz, c0:T], lhsT=negOnes[:, 0:ksz], rhs=Lsum[:, c0:T],
                                                               start=False, stop=False, skip_group_check=True),
                                      reads=arr(20, 1) + ["cstb"], writes=[PSn[cbk]])
                            if bi + 1 < nb:
                                SC.op("pool", lambda e: e.tensor_tensor(out=Lsum[0:ksz, c0:T], in0=Lsum[0:ksz, c0:T], in1=Lp[0:ksz, c0:T],
                                                                        op=ALU.add), reads=arr(20, 1) + arr(4 + zb, 1), writes=arr(20, 1))
                            SC.op("act", lambda e: e.activation(out=Xt[0:ksz, c0:T], in_=PS[cbk][0:ksz, c0:T], func=AF.Exp),
                                  reads=[PSn[cbk]], writes=arr(14 + 2 * zb, 2))
                            SC.op("dve", lambda e: e.tensor_tensor(out=Wt[0:ksz, c0:T], in0=Et[0:ksz, c0:T], in1=Xt[0:ksz, c0:T], op=ALU.mult),
                                  reads=arr(eb, 2) + arr(14 + 2 * zb, 2), writes=arr(6 + zb, 1))

                        def s2b(bi):
                            kb, ksz, kcol, c0, diag = blocks[bi]
                            zb = bi % 2
                            Wt = ar_bf(6 + zb, 512)
                            SC.op("pe", lambda e: e.matmul(PS[4][pr0:pr0 + 64, c0:T], lhsT=Vs[0:ksz, kb, hh * 64:(hh + 1) * 64],
                                                           rhs=Wt[0:ksz, c0:T], start=(bi == 0), stop=False, skip_group_check=True),
                                  reads=arr(6 + zb, 1) + ["Vs"], writes=["ps4"])

                        s1(0)
                        if nb > 1:
                            s1(1)
                        s2a(0)
                        for bi in range(nb):
                            if bi + 2 < nb:
                                s1(bi + 2)
                            if bi + 1 < nb:
                                s2a(bi + 1)
                            s2b(bi)
                    osb = ar_f32(8, 512)
                    sq = ar_bf(10, 512)
                    rr = ar_f32(11, 512)
                    SC.op("dve", lambda e: e.tensor_copy(out=osb[:, 0:T], in_=PS[4][:, 0:T]), reads=["ps4"], writes=arr(8, 2))
                    SC.op("pool", lambda e: e.tensor_tensor(out=sq[:, 0:T], in0=osb[:, 0:T], in1=osb[:, 0:T], op=ALU.mult),
                          reads=arr(8, 2), writes=arr(10, 1))
                    SC.op("pe", lambda e: e.matmul(PS[6][:, 0:T], lhsT=mean64, rhs=sq[:, 0:T], start=True, stop=True),
                          reads=arr(10, 1) + ["cstb"], writes=["ps6"])
                    SC.op("act", lambda e: e.activation(out=rr[:, 0:T], in_=PS[6][:, 0:T], func=AF.Ln, bias=epsc),
                          reads=["ps6", "small_eps"], writes=arr(11, 2))
                    SC.op("act", lambda e: e.activation(out=rr[:, 0:T], in_=rr[:, 0:T], func=AF.Exp, scale=-0.5),
                          reads=arr(11, 2), writes=arr(11, 2))
                    SC.op("dve", lambda e, p=p: e.scalar_tensor_tensor(out=mixT[:, 4 + p, 0:T], in0=osb[:, 0:T], scalar=sbwc, in1=rr[:, 0:T],
                                                                      op0=ALU.mult, op1=ALU.mult),
                          reads=arr(8, 2) + arr(11, 2) + ["small"], writes=["mixT"])

                chk('p3b')
                if dbg and not dbg_done[0]:
                    SC.dma(lambda e: e.dma_start(out=d_mix[:, :], in_=mixT[:].rearrange("p a b -> p (a b)")), "dbg", reads=["mixT"])
                wA = wget(6)
                wB = wget(7)
                prev = None
                for ti, (tok0, nt) in enumerate(tiles):
                    for half in range(2):
                        w = wA if half == 0 else wB
                        bank = 2 + half
                        for c in range(8):
                            SC.op("pe", lambda e, c=c, w=w, bank=bank, tok0=tok0, nt=nt: e.matmul(
                                PS[bank][0:nt, 0:512], lhsT=mixT[:, c, tok0:tok0 + nt], rhs=wb[w][:, c, :],
                                start=(c == 0), stop=(c == 7)), reads=["wb%d" % w, "mixT"], writes=[PSn[bank]])
                        SC.op("dve", lambda e, ti=ti, nt=nt, bank=bank, half=half: e.tensor_tensor(
                            out=xt[ti][0:nt, half * 512:(half + 1) * 512], in0=PS[bank][0:nt, 0:512],
                            in1=xt[ti][0:nt, half * 512:(half + 1) * 512], op=ALU.add),
                            reads=[PSn[bank], "xt%d" % ti], writes=["xt%d" % ti])
                    if prev is not None:
                        norm_transpose(xt[prev[0]], "xt%d" % prev[0], prev[1], prev[2], prev[0] % 2, 2 * prev[0], hT, "hT")
                    prev = (ti, tok0, nt)
                norm_transpose(xt[prev[0]], "xt%d" % prev[0], prev[1], prev[2], prev[0] % 2, 2 * prev[0], hT, "hT")
                wdone()
                wdone()

                chk('p4')
                if dbg and not dbg_done[0]:
                    for ti in range(4):
                        SC.dma(lambda e, ti=ti: e.dma_start(out=d_x1[ti * 128:(ti + 1) * 128, :], in_=xt[ti][:, :]), "dbg", reads=["xt%d" % ti])
                    SC.dma(lambda e: e.dma_start(out=d_hT[:, :], in_=hT[:].rearrange("p a b -> p (a b)")), "dbg", reads=["hT"])
                for k in range(11):
                    w = wget(8 + k)
                    for s_ in range(2):
                        j = 2 * k + s_
                        par = j % 2
                        bufs = []
                        for gv in range(2):
                            ch = j + gv * NJ
                            bank = gv + 4 * par
                            for c in range(8):
                                SC.op("pe", lambda e, c=c, bank=bank, gv=gv: e.matmul(
                                    PS[bank][:, 0:T], lhsT=wb[w][:, c, gv * 256 + s_ * 128: gv * 256 + (s_ + 1) * 128], rhs=hT[:, c, 0:T],
                                    start=(c == 0), stop=(c == 7)), reads=["wb%d" % w, "hT"], writes=[PSn[bank]])
                            s0 = par * 10 + gv * 3
                            us = AR[:, s0 * 256:s0 * 256 + 2 + T]
                            ures = arr(s0, 3)
                            c0_ = par * 10 + 6 + gv * 2
                            cc = ar_f32(c0_, 512)
                            cres = arr(c0_, 2)
                            SC.op("pool", lambda e, us=us, ch=ch: e.tensor_copy(out=us[:, 0:2], in_=halo[:, :, ch]),
                                  reads=["halo"], writes=ures)
                            SC.op("act", lambda e, us=us, bank=bank: e.activation(out=us[:, 2:2 + T], in_=PS[bank][:, 0:T], func=AF.Copy),
                                  reads=[PSn[bank]], writes=ures)
                            SC.op("act", lambda e, cc=cc, bank=bank, ch=ch: e.activation(
                                out=cc[:, 0:T], in_=PS[bank][:, 0:T], func=AF.Identity, scale=cw[:, 2, ch:ch + 1], bias=cb[:, ch:ch + 1]),
                                reads=[PSn[bank], "cw", "cb"], writes=cres)
                            SC.op("pool", lambda e, us=us, ch=ch: e.tensor_copy(out=halo[:, :, ch], in_=us[:, T:T + 2]),
                                  reads=ures, writes=["halo"])
                            SC.op("dve", lambda e, us=us, cc=cc, ch=ch: e.scalar_tensor_tensor(
                                out=cc[:, 0:T], in0=us[:, 1:1 + T], scalar=cw[:, 1, ch:ch + 1], in1=cc[:, 0:T],
                                op0=ALU.mult, op1=ALU.add), reads=ures + cres + ["cw"], writes=cres)
                            SC.op("dve", lambda e, us=us, cc=cc, ch=ch: e.scalar_tensor_tensor(
                                out=cc[:, 0:T], in0=us[:, 0:T], scalar=cw[:, 0, ch:ch + 1], in1=cc[:, 0:T],
                                op0=ALU.mult, op1=ALU.add), reads=ures + cres + ["cw"], writes=cres)
                            bufs.append((cc, cres))
                        (cg, gres), (cv, vres) = bufs
                        SC.op("act", lambda e, cg=cg: e.activation(out=cg[:, 0:T], in_=cg[:, 0:T], func=AF.Silu), reads=gres, writes=gres)
                        SC.op("pool", lambda e, cg=cg, cv=cv, j=j: e.tensor_tensor(out=gT[:, j, 0:T], in0=cg[:, 0:T], in1=cv[:, 0:T], op=ALU.mult),
                              reads=gres + vres, writes=["gTa"])
                    wdone()

                chk('p5')
                if dbg and not dbg_done[0]:
                    SC.dma(lambda e: e.dma_start(out=d_gT[:, :], in_=gT[:].rearrange("p a b -> p (a b)")), "dbg", reads=["gTa"])
                    dbg_done[0] = True
                for hf in range(2):
                    for jp in range(3):
                        w = wget(19 + hf * 3 + jp)
                        nj = 8 if jp < 2 else 6
                        pi_ = hf * 3 + jp
                        if nextU is not None and pi_ < len(nextU.tiles):
                            phase1_A(nextU, pi_)
                        for ti, (tok0, nt) in enumerate(tiles):
                            bank = 2 + ti
                            for jl in range(nj):
                                j = jp * 8 + jl
                                SC.op("pe", lambda e, w=w, jl=jl, j=j, bank=bank, tok0=tok0, nt=nt, jp=jp, nj=nj: e.matmul(
                                    PS[bank][0:nt, 0:512], lhsT=gT[:, j, tok0:tok0 + nt], rhs=wb[w][:, jl, :],
                                    start=(jp == 0 and jl == 0), stop=(jp == 2 and jl == nj - 1)),
                                    reads=["wb%d" % w, "gTa"], writes=[PSn[bank]])
                        wdone()
                        if nextU is not None and pi_ < len(nextU.tiles):
                            phase1_B(nextU, pi_)
                    for ti, (tok0, nt) in enumerate(tiles):
                        bank = 2 + ti
                        SC.op("dve", lambda e, ti=ti, nt=nt, bank=bank, hf=hf: e.tensor_tensor(
                            out=xt[ti][0:nt, hf * 512:(hf + 1) * 512], in0=PS[bank][0:nt, 0:512],
                            in1=xt[ti][0:nt, hf * 512:(hf + 1) * 512], op=ALU.add),
                            reads=[PSn[bank], "xt%d" % ti], writes=["xt%d" % ti])
                for ti, (tok0, nt) in enumerate(tiles):
                    col = 8 + 2 * ti
                    rms_rstd(xt[ti], nt, "xt%d" % ti, col, hb[ti % 2], "hb%d" % (ti % 2))
                    SC.op("dve", lambda e, ti=ti, nt=nt, col=col: e.scalar_tensor_tensor(
                        out=xt[ti][0:nt, :], in0=xt[ti][0:nt, :], scalar=stat[0:nt, col:col + 1], in1=fin_b[0:nt, :],
                        op0=ALU.mult, op1=ALU.mult), reads=["xt%d" % ti, "stat%d" % col, "fin_b"], writes=["xt%d" % ti])
                    SC.dma(lambda e, ti=ti, tok0=tok0, nt=nt: e.dma_start(out=U.y_dst(tok0, nt), in_=xt[ti][0:nt, :]),
                           "o_xt%d" % ti, reads=["xt%d" % ti], q="pool")

            def conv_state_out(dst):
                hv = halo[:].rearrange("p a b -> p (a b)")
                SC.op("pe", lambda e: e.transpose(out=PS[5][0:88, 0:128], in_=hv[:, 0:88], identity=identf[:, :]),
                      reads=["halo", "identf"], writes=["ps5"])
                cs = ar_f32(12, 128)
                SC.op("dve", lambda e: e.tensor_copy(out=cs[0:88, :], in_=PS[5][0:88, 0:128]), reads=["ps5"], writes=arr(12, 1))
                for i in range(2):
                    SC.dma(lambda e, i=i: e.dma_start(out=dst[i, :].rearrange("(a b) -> a b", b=128), in_=cs[i * 44:(i + 1) * 44, :]),
                           "cso", reads=arr(12, 1))

            n_units = NSEQ * NU + 1
            for _ in range(n_units):
                seq_pieces.extend(range(25))
            for _ in range(NWB):
                wload_next()

            def prompt_near(h, kb, qb0, nqb):
                near = []
                for ty in range(2):
                    qb = kb + ty
                    if qb0 <= qb < qb0 + nqb:
                        near.append(((qb - qb0) * 128, 128, Hb[:, h, ty, :], Jm))
                return near

            units = []
            for b in range(NSEQ):
                for G in range(NU):
                    U = Unit()
                    U.T = 512
                    U.tiles = [(i * 128, 128) for i in range(4)]
                    U.kpos0 = G * 512
                    U.kb0 = G * 4
                    U.x_src = lambda tok0, nt, b=b, G=G: xp[b, G * 512 + tok0:G * 512 + tok0 + nt, :]
                    U.y_dst = lambda tok0, nt, b=b, G=G: yp[b, G * 512 + tok0:G * 512 + tok0 + nt, :]

                    def mk_out(dst, nh, dh, b=b, G=G):
                        def f(tok0, nt, sg):
                            p0 = G * 512 + tok0
                            return [(dst[b, :, p0:p0 + nt, :].rearrange("h s d -> s h d"),
                                     sg[0:nt, :].rearrange("p (h d) -> p h d", h=nh))]
                        return f
                    U.out_dk = mk_out(pdk, 4, 128)
                    U.out_dv = mk_out(pdv, 4, 128)
                    U.out_sk = mk_out(psk, 8, 64)
                    U.out_sv = mk_out(psv, 8, 64)

                    def diff_blocks(q0, W, h, G=G):
                        qb0 = (G * 512 + q0) // 128
                        nqb = W // 128
                        out = []
                        for kb in range(qb0 + nqb):
                            c0 = max(0, (kb - qb0) * 128)
                            out.append((kb, 128, kb * 128, c0, prompt_near(h, kb, qb0, nqb)))
                        return out
                    U.diff_blocks = diff_blocks

                    def sb_blocks(G=G):
                        qb0 = G * 4
                        out = []
                        for kb in range(qb0 + 3, -1, -1):
                            c0 = max(0, (kb - qb0) * 128)
                            diag = (128, HsbM, Jm) if kb >= qb0 else None
                            out.append((kb, 128, kb * 128, c0, diag))
                        return out
                    U.sb_blocks = sb_blocks
                    units.append((U, "p", b, G))

            def sample_pre():
                SC.op("pool", lambda e: e.memset(halo[:], 0.0), reads=["halo"], writes=["halo"])
                cache_jobs = []
                for h in range(4):
                    cache_jobs.append(("k", [(cdk[h], 0, 128)], kT[h], "kT%d" % h))
                for p in range(4):
                    cache_jobs.append(("k", [(csk[2 * p], 0, 64), (csk[2 * p + 1], 64, 64)], kT[4 + p], "kT%d" % (4 + p)))
                for h in range(4):
                    cache_jobs.append(("v", [(cdv[h], 0, 128)], (Vd, h * 128, 128), "Vd"))
                for hh in range(8):
                    cache_jobs.append(("v", [(csv[hh], 0, 64)], (Vs, hh * 64, 64), "Vs"))
                for ji, (kind, srcs, dst, dres) in enumerate(cache_jobs):
                    sl = ji % 2
                    cst_f = ar_f32(sl * 8, 2048).rearrange("p (a b) -> p a b", a=16)
                    cres = arr(sl * 8, 8)
                    wd_tot = sum(s[2] for s in srcs)
                    for (src, coff, wd) in srcs:
                        SC.dma(lambda e, src=src, coff=coff, wd=wd, cst_f=cst_f: e.dma_start(
                            out=cst_f[:, :, coff:coff + wd], in_=src.rearrange("(a p) d -> p a d", p=128)), "cst%d" % sl, writes=cres)
                    if kind == "v":
                        vt, vo, vw = dst
                        SC.op("pool" if ji % 2 else "dve", lambda e, vt=vt, vo=vo, vw=vw, cst_f=cst_f: e.tensor_copy(
                            out=vt[:, 0:16, vo:vo + vw], in_=cst_f[:, :, 0:vw]), reads=cres, writes=[dres])
                    else:
                        bres = arr(16, 4)
                        cbf = ar_bf(16, 2048).rearrange("p (a b) -> p a b", a=16)
                        SC.op("dve", lambda e, cbf=cbf, cst_f=cst_f: e.tensor_copy(out=cbf[:, :, :], in_=cst_f[:, :, :]), reads=cres, writes=bres)
                        for half in range(2):
                            pT = PS[half][:].bitcast(BF16).rearrange("p (a b) -> p a b", a=8)
                            for a in range(8):
                                SC.op("pe", lambda e, pT=pT, a=a, half=half, cbf=cbf: e.transpose(out=pT[:, a, :], in_=cbf[:, half * 8 + a, :], identity=ident),
                                      reads=bres + ["cstb"], writes=[PSn[half]])
                            SC.op("act" if half else "dve", (lambda e, pT=pT, half=half, dst=dst: e.activation(
                                out=dst[:, half * 1024:(half + 1) * 1024], in_=pT[:].rearrange("p a b -> p (a b)"), func=AF.Copy)) if half else
                                (lambda e, pT=pT, half=half, dst=dst: e.tensor_copy(out=dst[:, half * 1024:(half + 1) * 1024],
                                                                                   in_=pT[:].rearrange("p a b -> p (a b)"))),
                                reads=[PSn[half]], writes=[dres])
                for i in range(2):
                    SC.dma(lambda e, i=i: e.dma_start(out=halo[:, i, :], in_=dap(sconv, i * 2 * DFF, [[1, 128], [128, 44]]),
                                                     allow_slow_non_contiguous=True), "prm", reads=["halo"], writes=["halo"])

            U = Unit()
            U.T = DEC
            U.tiles = [(0, DEC)]
            U.kpos0 = PAST
            U.kb0 = 16
            U.x_src = lambda tok0, nt: xs[tok0:tok0 + nt, :]
            U.y_dst = lambda tok0, nt: ys[tok0:tok0 + nt, :]

            def mk_out_s(dst, nh):
                def f(tok0, nt, sg):
                    return [(dst[:, tok0:tok0 + nt, :].rearrange("h s d -> s h d"), sg[0:nt, :].rearrange("p (h d) -> p h d", h=nh))]
                return f
            U.out_dk = mk_out_s(sdk, 4)
            U.out_dv = mk_out_s(sdv, 4)
            U.out_sk = mk_out_s(ssk, 8)
            U.out_sv = mk_out_s(ssv, 8)

            def diff_blocks_s(q0, W, h):
                out = []
                for kb in range(16):
                    near = [(0, 16, Hb[:, h, 1, 0:16], Jm)] if kb == 15 else []
                    out.append((kb, 128, kb * 128, 0, near))
                out.append((16, 16, PAST, 0, [(0, 16, Hb16[:, h, :], J16)]))
                return out
            U.diff_blocks = diff_blocks_s

            def sb_blocks_s():
                out = [(16, 16, PAST, 0, (16, Hsb16, J16))]
                for kb in range(15, -1, -1):
                    out.append((kb, 128, kb * 128, 0, None))
                return out
            U.sb_blocks = sb_blocks_s
            units.append((U, "s", 0, 0))

            for ti in range(len(units[0][0].tiles)):
                phase1_tile(units[0][0], ti)
            for idx, (U, kind, b, G) in enumerate(units):
                nextU = units[idx + 1][0] if idx + 1 < len(units) else None
                if kind == "p":
                    if G == 0 and b > 0:
                        SC.op("pool", lambda e: e.memset(halo[:], 0.0), reads=["halo"], writes=["halo"])
                    run_unit(U, nextU)
                    if G == NU - 1:
                        conv_state_out(pcv[b])
                else:
                    chk('prompt')
                    sample_pre()
                    run_unit(U, None)
                    conv_state_out(scv)

        except _Stop:
            pass
        SC.finish()
    return nc


def _run(nc, in_maps):
    return run_bass_kernel_spmd(nc, in_maps, core_ids=list(range(len(in_maps))))


def kernel(x_prompt, x_sample, cache_diff_k, cache_diff_v, cache_sb_k, cache_sb_v, state_conv,
           attn_norm_w, w_in, lambda_q1, lambda_k1, lambda_q2, lambda_k2, diff_subln_w,
           sb_norm_w, w_out, ffn_norm_w, w_up, conv_w, conv_b, w_down, rel_bias, final_norm_w):
    NCORES = 8
    f = lambda a: np.ascontiguousarray(np.asarray(a, dtype=np.float32))
    B, S, _ = x_prompt.shape
    NSEQ = B // NCORES
    oh, cst = _constants()
    shared = {
        "w_in": f(w_in[0]), "w_out": f(w_out[0]), "w_up": f(w_up[0]), "w_down": f(w_down[0]),
        "anw": f(attn_norm_w[0]), "fnw": f(ffn_norm_w[0]), "finw": f(final_norm_w),
        "sublnw": f(diff_subln_w[0]), "sbw": f(sb_norm_w[0]),
        "lam4": f(np.stack([np.asarray(lambda_q1[0]), np.asarray(lambda_k1[0]), np.asarray(lambda_q2[0]), np.asarray(lambda_k2[0])])),
        "conv_w": f(conv_w[0]), "conv_b": f(conv_b[0]), "relb": f(rel_bias),
        "ohc": f(oh.reshape(32, 768)), "cstc": f(cst.reshape(128, 1408)),
    }
    xp = f(x_prompt)
    in_maps = []
    for c in range(NCORES):
        m = dict(shared)
        m["xp"] = xp[c * NSEQ:(c + 1) * NSEQ]
        m["xs"] = f(x_sample[c])
        m["cdk"] = f(cache_diff_k[0, c])
        m["cdv"] = f(cache_diff_v[0, c])
        m["csk"] = f(cache_sb_k[0, c])
        m["csv"] = f(cache_sb_v[0, c])
        m["sconv"] = f(state_conv[0, c])
        in_maps.append(m)
    nc = build_program(NSEQ, S)
    res = _run(nc, in_maps).results
    cat = lambda k: np.concatenate([r[k] for r in res], axis=0)
    stk = lambda k: np.stack([r[k] for r in res], axis=0)
    return (cat("yp"), stk("ys"),
            cat("pdk")[None], cat("pdv")[None], cat("psk")[None], cat("psv")[None], cat("pcv")[None],
            stk("sdk")[None], stk("sdv")[None], stk("ssk")[None], stk("ssv")[None], stk("scv")[None])
```
